# Optimizing a Trainium2 kernel written in Bass

```python
import math
import jax, jax.numpy as jnp
from jax import lax
import numpy as np

D_MODEL = 1024
BATCH = 4
SEQ = 4096
DEPTH = 1

MEM_LEN = 256
EPS = 1e-6

SSD_HEADS = 16
SSD_HEAD_DIM = 64
SSD_WIDTH = SSD_HEADS * SSD_HEAD_DIM
SSD_GROUPS = 4
SSD_STATE = 128
CONV_WIDTH = 4
SSD_CHUNK = 128
CONV_CH = SSD_WIDTH + 2 * SSD_GROUPS * SSD_STATE

DA_HEADS = 8
DA_HEAD_DIM = 64
DA_V_DIM = 2 * DA_HEAD_DIM
DA_WIDTH = DA_HEADS * DA_V_DIM
Q_BLOCK = 128

NUM_BUCKETS = 32
MAX_DISTANCE = 128

IN_SIZES = (SSD_WIDTH, CONV_CH, SSD_HEADS, DA_WIDTH, DA_WIDTH, DA_WIDTH)
IN_WIDTH = SSD_WIDTH + CONV_CH + SSD_HEADS + 3 * DA_WIDTH
MIX_WIDTH = SSD_WIDTH + DA_WIDTH

CROSS_HEADS = 4
CROSS_HEAD_DIM = D_MODEL // CROSS_HEADS

D_FF = 2816

kernel_name = 'hymba_ssd_diffattn_macaron_layer'


def rms_norm(x, w):
    x32 = x.astype(jnp.float32)
    y = x32 * lax.rsqrt(jnp.mean(x32 * x32, axis=-1, keepdims=True) + EPS)
    return y.astype(x.dtype) * w


def swiglu(h, w_in, w_out):
    gate, up = jnp.split(h @ w_in, 2, axis=-1)
    return (jax.nn.silu(gate) * up) @ w_out


def split_sizes(u, sizes):
    idx, acc = [], 0
    for s in sizes[:-1]:
        acc += s
        idx.append(acc)
    return jnp.split(u, idx, axis=-1)


def causal_dwconv(u, w, b):
    y = lax.conv_general_dilated(u, w[:, None, :], window_strides=(1,),
                                 padding=[(CONV_WIDTH - 1, 0)],
                                 dimension_numbers=('NWC', 'WIO', 'NWC'),
                                 feature_group_count=u.shape[-1])
    return y + b


def ssd_scan(xh, dt, a, b_in, c_in):
    bsz, seqlen, nh, hp = xh.shape
    ng, ns = b_in.shape[2], b_in.shape[3]
    r = nh // ng
    nc = seqlen // SSD_CHUNK
    xdt = (xh * dt[..., None]).reshape(bsz, nc, SSD_CHUNK, ng, r, hp)
    la = (dt * a).reshape(bsz, nc, SSD_CHUNK, ng, r)
    bc = b_in.reshape(bsz, nc, SSD_CHUNK, ng, ns)
    cc = c_in.reshape(bsz, nc, SSD_CHUNK, ng, ns)
    la_cum = jnp.cumsum(la, axis=2)
    causal = jnp.tril(jnp.ones((SSD_CHUNK, SSD_CHUNK), dtype=bool))
    seg = la_cum[:, :, :, None] - la_cum[:, :, None, :]
    decay = jnp.exp(jnp.where(causal[None, None, :, :, None, None], seg, -jnp.inf))
    cb = jnp.einsum('bclgn,bcsgn->bclsg', cc, bc)
    y_diag = jnp.einsum('bclsg,bclsgr,bcsgrp->bclgrp', cb, decay, xdt)
    decay_to_end = jnp.exp(la_cum[:, :, -1:] - la_cum)
    chunk_states = jnp.einsum('bclgn,bclgr,bclgrp->bcgrpn', bc, decay_to_end, xdt)
    chunk_decay = jnp.exp(la_cum[:, :, -1])

    def step(state, inp):
        st, dec = inp
        return state * dec[..., None, None] + st, state

    init = jnp.zeros_like(chunk_states[:, 0])
    _, prev = lax.scan(step, init, (jnp.moveaxis(chunk_states, 1, 0),
                                    jnp.moveaxis(chunk_decay, 1, 0)))
    prev = jnp.moveaxis(prev, 0, 1)
    y_off = jnp.einsum('bclgn,bcgrpn,bclgr->bclgrp', cc, prev, jnp.exp(la_cum))
    return (y_diag + y_off).reshape(bsz, seqlen, nh, hp)


def rel_bucket(rel):
    n = jnp.maximum(-rel, 0)
    max_exact = NUM_BUCKETS // 2
    nf = jnp.maximum(n, 1).astype(jnp.float32)
    large = max_exact + (jnp.log(nf / max_exact) / math.log(MAX_DISTANCE / max_exact)
                         * (NUM_BUCKETS - max_exact)).astype(jnp.int32)
    large = jnp.minimum(large, NUM_BUCKETS - 1)
    return jnp.where(n < max_exact, n, large)


def diff_attention(q, k, v, lam, rel_bias):
    bsz, seqlen, nh, _, dh = q.shape
    nb = seqlen // Q_BLOCK
    scale = dh ** -0.5
    qb = jnp.moveaxis(q.reshape(bsz, nb, Q_BLOCK, nh, 2, dh), 1, 0)
    k_pos = jnp.arange(seqlen)

    def block(args):
        q_blk, i = args
        q_pos = i * Q_BLOCK + jnp.arange(Q_BLOCK)
        rel = k_pos[None, :] - q_pos[:, None]
        bias = jnp.transpose(rel_bias[rel_bucket(rel)], (2, 0, 1)).astype(jnp.float32)
        logits = jnp.einsum('bqhcd,bkhcd->bhcqk', q_blk, k,
                            preferred_element_type=jnp.float32) * scale
        logits = logits + bias[None, :, None]
        logits = jnp.where(rel[None, None, None] <= 0, logits, -jnp.inf)
        p = jax.nn.softmax(logits, axis=-1)
        w = p[:, :, 0] - lam * p[:, :, 1]
        return jnp.einsum('bhqk,bkhe->bqhe', w.astype(v.dtype), v,
                          preferred_element_type=jnp.float32)

    out = lax.map(block, (qb, jnp.arange(nb)))
    return jnp.moveaxis(out, 0, 1).reshape(bsz, seqlen, nh, 2 * dh)


def hybrid_mixer(h, w_in, conv_w, conv_b, dt_bias, a_log, d_skip, ssd_norm_w,
                 lq1, lk1, lq2, lk2, subln_w, rel_bias, w_out, lambda_init):
    bsz, seqlen, _ = h.shape
    z, xbc, dt_raw, q, k, v = split_sizes(h @ w_in, IN_SIZES)

    xbc = jax.nn.silu(causal_dwconv(xbc, conv_w, conv_b))
    xs, bs, cs = jnp.split(xbc, [SSD_WIDTH, SSD_WIDTH + SSD_GROUPS * SSD_STATE], axis=-1)
    dt = jax.nn.softplus(dt_raw.astype(jnp.float32) + dt_bias.astype(jnp.float32))
    a = -jnp.exp(a_log.astype(jnp.float32))
    xh = xs.reshape(bsz, seqlen, SSD_HEADS, SSD_HEAD_DIM).astype(jnp.float32)
    y = ssd_scan(xh, dt, a,
                 bs.reshape(bsz, seqlen, SSD_GROUPS, SSD_STATE).astype(jnp.float32),
                 cs.reshape(bsz, seqlen, SSD_GROUPS, SSD_STATE).astype(jnp.float32))
    y = y + d_skip.astype(jnp.float32)[:, None] * xh
    y = y.reshape(bsz, seqlen, SSD_WIDTH) * jax.nn.silu(z.astype(jnp.float32))
    yg = y.reshape(bsz, seqlen, SSD_GROUPS, SSD_WIDTH // SSD_GROUPS)
    yg = yg * lax.rsqrt(jnp.mean(yg * yg, axis=-1, keepdims=True) + EPS)
    y_ssd = yg.reshape(bsz, seqlen, SSD_WIDTH).astype(h.dtype) * ssd_norm_w

    q = q.reshape(bsz, seqlen, DA_HEADS, 2, DA_HEAD_DIM)
    k = k.reshape(bsz, seqlen, DA_HEADS, 2, DA_HEAD_DIM)
    v = v.reshape(bsz, seqlen, DA_HEADS, DA_V_DIM)
    lam = (jnp.exp(jnp.sum(lq1.astype(jnp.float32) * lk1.astype(jnp.float32)))
           - jnp.exp(jnp.sum(lq2.astype(jnp.float32) * lk2.astype(jnp.float32)))
           + lambda_init)
    o = diff_attention(q, k, v, lam, rel_bias)
    o = o * lax.rsqrt(jnp.mean(o * o, axis=-1, keepdims=True) + EPS)
    y_attn = (o.astype(h.dtype) * subln_w * (1.0 - lambda_init)).reshape(bsz, seqlen, DA_WIDTH)

    return jnp.concatenate([y_ssd, y_attn], axis=-1) @ w_out


def memory_cross_attention(h, mem_n, w_cq, w_ck, w_cv, w_co):
    bsz, seqlen, _ = h.shape
    mlen = mem_n.shape[1]
    q = (h @ w_cq).reshape(bsz, seqlen, CROSS_HEADS, CROSS_HEAD_DIM)
    k = (mem_n @ w_ck).reshape(bsz, mlen, CROSS_HEADS, CROSS_HEAD_DIM)
    v = (mem_n @ w_cv).reshape(bsz, mlen, CROSS_HEADS, CROSS_HEAD_DIM)
    logits = jnp.einsum('bqhd,bkhd->bhqk', q, k,
                        preferred_element_type=jnp.float32) * (CROSS_HEAD_DIM ** -0.5)
    p = jax.nn.softmax(logits, axis=-1)
    o = jnp.einsum('bhqk,bkhd->bqhd', p.astype(v.dtype), v).reshape(bsz, seqlen, D_MODEL)
    return o @ w_co


def setup_inputs(seed: int = 0) -> dict:
    key = jax.random.key(seed)
    ks = jax.random.split(key, 32)
    f32 = jnp.float32

    def nrm(k, shape, scale):
        return jax.random.normal(k, shape, f32) * scale

    def gain(k, shape):
        return 1.0 + 0.02 * jax.random.normal(k, shape, f32)

    L = DEPTH
    dt0 = jnp.exp(jax.random.uniform(ks[10], (L, SSD_HEADS), f32,
                                     math.log(1e-3), math.log(1e-1)))
    dt_bias = dt0 + jnp.log(-jnp.expm1(-dt0))
    a_log = jnp.log(jax.random.uniform(ks[11], (L, SSD_HEADS), f32, 1.0, 16.0))
    return {
        'x': jax.random.normal(ks[0], (BATCH, SEQ, D_MODEL), f32),
        'mem': jax.random.normal(ks[1], (BATCH, MEM_LEN, D_MODEL), f32),
        'norm_ffn1_w': gain(ks[2], (L, D_MODEL)),
        'ffn1_w_in': nrm(ks[3], (L, D_MODEL, 2 * D_FF), D_MODEL ** -0.5),
        'ffn1_w_out': nrm(ks[4], (L, D_FF, D_MODEL), D_FF ** -0.5),
        'norm_mix_w': gain(ks[5], (L, D_MODEL)),
        'w_in_mix': nrm(ks[6], (L, D_MODEL, IN_WIDTH), D_MODEL ** -0.5),
        'conv_w': nrm(ks[7], (L, CONV_WIDTH, CONV_CH), CONV_WIDTH ** -0.5),
        'conv_b': nrm(ks[8], (L, CONV_CH), 0.01),
        'dt_bias': dt_bias,
        'a_log': a_log,
        'd_skip': gain(ks[9], (L, SSD_HEADS)),
        'ssd_norm_w': gain(ks[12], (L, SSD_WIDTH)),
        'lambda_q1': nrm(ks[13], (L, DA_HEAD_DIM), 0.1),
        'lambda_k1': nrm(ks[14], (L, DA_HEAD_DIM), 0.1),
        'lambda_q2': nrm(ks[15], (L, DA_HEAD_DIM), 0.1),
        'lambda_k2': nrm(ks[16], (L, DA_HEAD_DIM), 0.1),
        'subln_w': gain(ks[17], (L, DA_V_DIM)),
        'rel_bias': nrm(ks[18], (NUM_BUCKETS, DA_HEADS), 0.2),
        'w_out_mix': nrm(ks[19], (L, MIX_WIDTH, D_MODEL), MIX_WIDTH ** -0.5),
        'norm_cross_w': gain(ks[20], (L, D_MODEL)),
        'norm_mem_w': gain(ks[21], (L, D_MODEL)),
        'w_cq': nrm(ks[22], (L, D_MODEL, D_MODEL), D_MODEL ** -0.5),
        'w_ck': nrm(ks[23], (L, D_MODEL, D_MODEL), D_MODEL ** -0.5),
        'w_cv': nrm(ks[24], (L, D_MODEL, D_MODEL), D_MODEL ** -0.5),
        'w_co': nrm(ks[25], (L, D_MODEL, D_MODEL), D_MODEL ** -0.5),
        'norm_ffn2_w': gain(ks[26], (L, D_MODEL)),
        'ffn2_w_in': nrm(ks[27], (L, D_MODEL, 2 * D_FF), D_MODEL ** -0.5),
        'ffn2_w_out': nrm(ks[28], (L, D_FF, D_MODEL), D_FF ** -0.5),
        'norm_final_w': gain(ks[29], (D_MODEL,)),
    }


def reference(x, mem, norm_ffn1_w, ffn1_w_in, ffn1_w_out, norm_mix_w, w_in_mix,
              conv_w, conv_b, dt_bias, a_log, d_skip, ssd_norm_w,
              lambda_q1, lambda_k1, lambda_q2, lambda_k2, subln_w, rel_bias,
              w_out_mix, norm_cross_w, norm_mem_w, w_cq, w_ck, w_cv, w_co,
              norm_ffn2_w, ffn2_w_in, ffn2_w_out, norm_final_w):
    for l in range(DEPTH):
        lambda_init = 0.8 - 0.6 * math.exp(-0.3 * l)
        x = x + 0.5 * swiglu(rms_norm(x, norm_ffn1_w[l]), ffn1_w_in[l], ffn1_w_out[l])
        x = x + hybrid_mixer(rms_norm(x, norm_mix_w[l]), w_in_mix[l], conv_w[l], conv_b[l],
                             dt_bias[l], a_log[l], d_skip[l], ssd_norm_w[l],
                             lambda_q1[l], lambda_k1[l], lambda_q2[l], lambda_k2[l],
                             subln_w[l], rel_bias, w_out_mix[l], lambda_init)
        x = x + memory_cross_attention(rms_norm(x, norm_cross_w[l]),
                                       rms_norm(mem, norm_mem_w[l]),
                                       w_cq[l], w_ck[l], w_cv[l], w_co[l])
        x = x + 0.5 * swiglu(rms_norm(x, norm_ffn2_w[l]), ffn2_w_in[l], ffn2_w_out[l])
    return rms_norm(x, norm_final_w)
```

```python
import numpy as np
from contextlib import ExitStack
import concourse.bass as bass
import concourse.mybir as mybir
from concourse.bass_utils import run_bass_kernel_spmd

F32 = mybir.dt.float32
BF16 = mybir.dt.bfloat16
AF = mybir.ActivationFunctionType
ALU = mybir.AluOpType
AX = mybir.AxisListType

EPS = 1e-6
NEG = -30000.0
LAMBDA_INIT = 0.2

C_G1, C_GF, C_G2, C_GM, C_GC, C_GMEM, C_SSDW, C_SUBLN, C_FLAG = 0, 8, 16, 24, 32, 40, 48, 56, 57
C_CONVB, C_CONVW, C_DTB, C_ALOG, C_DSKIP = 58, 74, 138, 154, 170
C_LQ1, C_LK1, C_LQ2, C_LK2, C_FAROWN, C_FARPRE, NCST = 186, 250, 314, 378, 442, 450, 464
O_Z, O_XBC, O_DT, O_Q, O_K, O_V = 0, 1024, 3072, 3088, 4112, 5136


class _Op:
    __slots__ = ("eng", "fn", "deps", "sem", "val", "needs", "idx", "dma")


class Sched:
    ENGS = ("pe", "act", "dve", "pool", "sp")

    def __init__(self):
        self.ops = {e: [] for e in self.ENGS}
        self.lastw = {}
        self.readers = {}
        self.dmacount = {}
        self.dmasems = {}
        self.bar = None
        self.synced = set(self.ENGS)

    def barrier(self):
        b = []
        for e in self.ENGS:
            last = None
            for o in reversed(self.ops[e]):
                if not o.dma:
                    last = o
                    break
            if last is not None:
                b.append(last)
        for sid, cnt in self.dmacount.items():
            b.append((self.dmasems[sid], cnt))
        self.bar = b
        self.synced = set()

    def add(self, eng, fn, reads=(), writes=(), dma_sem=None):
        o = _Op()
        o.eng, o.fn, o.needs, o.dma = eng, fn, False, dma_sem is not None
        o.idx = len(self.ops[eng])
        o.sem, o.val = dma_sem, 0
        deps = {}

        def dep(p, war):
            if p is o:
                return
            if isinstance(p, tuple):
                q = deps.get(("d", id(p[0])))
                if q is None or q[1] < p[1]:
                    deps[("d", id(p[0]))] = p
                return
            if p.dma:
                dep((p.sem, self.dmacount[id(p.sem)]), war)
                return
            if p.eng == eng and not o.dma and eng == "pe":
                return
            key = ("c", p.eng)
            q = deps.get(key)
            if q is None or p.idx > q.idx:
                deps[key] = p

        if eng not in self.synced:
            self.synced.add(eng)
            for p in self.bar:
                if isinstance(p, tuple) or p.eng != eng:
                    dep(p, False)
        for k in reads:
            w = self.lastw.get(k)
            if w is not None:
                dep(w, False)
            if isinstance(k, tuple) and k[0] == "ps":
                for r in self.readers.get(k, {}).values():
                    if r.eng != eng:
                        dep(r, True)
        for k in writes:
            w = self.lastw.get(k)
            if w is not None:
                dep(w, False)
            for r in self.readers.get(k, {}).values():
                dep(r, True)
        if o.dma:
            self.dmasems[id(dma_sem)] = dma_sem
            self.dmacount[id(dma_sem)] = self.dmacount.get(id(dma_sem), 0) + 16
            o.val = self.dmacount[id(dma_sem)]
        for k in writes:
            self.lastw[k] = o
            self.readers[k] = {}
        for k in reads:
            d = self.readers.setdefault(k, {})
            key = ("d", id(o.sem)) if o.dma else ("c", eng)
            d[key] = o
        o.deps = list(deps.values())
        for p in o.deps:
            if not isinstance(p, tuple):
                p.needs = True
        self.ops[eng].append(o)
        return o

    def emit(self, nc, block, engsem):
        for e in self.ENGS:
            c = 0
            for o in self.ops[e]:
                if not o.dma and o.needs:
                    c += 1
                    o.val = c
                    o.sem = engsem[e]

        def run(ename, eng):
            waited = {}
            for o in self.ops[ename]:
                for p in o.deps:
                    psem, pval = p if isinstance(p, tuple) else (p.sem, p.val)
                    if waited.get(id(psem), 0) < pval:
                        eng.wait_ge(psem, pval)
                        waited[id(psem)] = pval
                ins = o.fn(eng)
                if o.dma:
                    ins.then_inc(o.sem, 16)
                elif o.needs:
                    ins.then_inc(o.sem, 1)

        block.tensor(lambda eng: run("pe", eng))
        block.scalar(lambda eng: run("act", eng))
        block.vector(lambda eng: run("dve", eng))
        block.gpsimd(lambda eng: run("pool", eng))
        block.sync(lambda eng: run("sp", eng))


DBG = []


class _Stop(Exception):
    pass


class Rot:
    def __init__(self, items):
        self.items = list(items)
        self.i = 0

    def next(self):
        x = self.items[self.i % len(self.items)]
        self.i += 1
        return x


class Cfg:
    def __init__(self, npre=2048, nown=2048, tffn=1024, upto=9):
        self.D = 1024
        self.DFF = 2816
        self.NPRE = npre
        self.NOWN = nown
        self.NTOK = npre + nown
        self.TFFN = tffn
        self.upto = upto


def build(cfg):
    nc = bass.Bass("TRN2", target_bir_lowering=False)
    cfg._nc = nc
    D, DFF = cfg.D, cfg.DFF
    NPRE, NOWN, NTOK, T = cfg.NPRE, cfg.NOWN, cfg.NTOK, cfg.TFFN
    KC = D // 128
    FC = DFF // 128
    NS = T // 512
    NB = NTOK // 128
    NBP = NPRE // 128
    NTT = NTOK // 512
    NTO = NOWN // 512
    MEM = 256
    upto = cfg.upto

    def din(name, shape):
        return nc.dram_tensor(name, list(shape), F32, kind="ExternalInput").ap()

    xT_d = din("xT", [D, NTOK])
    memT_d = din("memT", [D, MEM])
    cst_d = din("cst", [128, NCST])
    bias_d = din("biasT", [128, 8 * 3 * 128])
    w1i_d = din("ffn1_w_in", [D, 2 * DFF])
    w1o_d = din("ffn1_w_out", [DFF, D])
    w2i_d = din("ffn2_w_in", [D, 2 * DFF])
    w2o_d = din("ffn2_w_out", [DFF, D])
    wim_d = din("w_in_mix", [D, 6160])
    wom_d = din("w_out_mix", [2 * D, D])
    wcq_d = din("w_cq", [D, D])
    wck_d = din("w_ck", [D, D])
    wcv_d = din("w_cv", [D, D])
    wco_d = din("w_co", [D, D])
    out_d = nc.dram_tensor("outT", [D, NOWN], F32, kind="ExternalOutput").ap()

    S = Sched()
    es = ExitStack()

    def sb(name, shape, dt):
        return es.enter_context(nc.sbuf_tensor("sb_" + name, list(shape), dt))

    def sem(name):
        return es.enter_context(nc.semaphore(name))

    with es:
        HW = max(NTOK + max(0, 2 * T - NOWN), 2048 + NOWN)
        xT = sb("xT_own", [128, KC, NOWN], F32)
        hTm_t = sb("hTm", [128, KC, HW], BF16)
        hTm = hTm_t[:]
        cst = sb("cst", [128, NCST], F32)
        onesm = sb("onesm", [128, 128], BF16)
        onesb = sb("onesb", [128, 128], BF16)
        onesf = sb("onesf", [128, 128], F32)
        ident = sb("ident", [128, 128], BF16)
        identf = sb("identf", [128, 128], F32)
        Umat = sb("Umat", [128, 128], F32)
        Lmat = sb("Lmat", [128, 128], F32)
        sq = sb("sq", [128, KC, 512], BF16)
        rstd = sb("rstd", [128, 512], F32)
        rtmp = sb("rtmp", [128, 512], F32)
        ARW = 16128
        arena = sb("arena", [128, ARW], F32)
        psp = [es.enter_context(nc.psum_tensor(f"psp{i}", [128, 1024], F32)) for i in range(4)]

        class _PsView:
            def __init__(self, ap):
                self.ap = ap

            def __getitem__(self, key):
                return self.ap[key]
        ps = [_PsView(psp[i // 2][:, (i % 2) * 512:(i % 2 + 1) * 512]) for i in range(8)]

        class Bump:
            def __init__(self):
                self.off = 0

            def __call__(self, shape, dt):
                n = 1
                for v in shape[1:]:
                    n *= v
                esz = 2 if dt is BF16 else 4
                nbytes = (n * esz + 31) // 32 * 32
                a = arena[:, self.off // 4:(self.off + nbytes) // 4]
                DBG.append((self.off, list(shape), "bf16" if dt is BF16 else "f32"))
                self.off += nbytes
                assert self.off <= ARW * 4, ("arena overflow", self.off)
                if dt is BF16:
                    a = a.bitcast(BF16)
                if nbytes // esz != n:
                    a = a[:, 0:n]
                if len(shape) == 3:
                    a = a.rearrange("p (a b) -> p a b", a=shape[1])
                elif len(shape) == 4:
                    a = a.rearrange("p (a b c) -> p a b c", a=shape[1], b=shape[2])
                return a

        engsem = {e: sem(f"sem_{e}") for e in Sched.ENGS}
        s_x = [sem(f"s_x{i}") for i in range(NTO)]
        s_cst = sem("s_cst")
        s_out = [sem(f"s_out{i}") for i in range(2)]
        psrot = Rot(range(8))

        xT_v = xT_d.rearrange("(c p) t -> p c t", p=128)
        out_v = out_d.rearrange("(c p) t -> p c t", p=128)
        wim_v = wim_d.rearrange("(c p) n -> p c n", p=128)
        wom_v = wom_d.rearrange("(f p) n -> p f n", p=128)

        S.add("sp", lambda e: e.dma_start(out=cst[:], in_=cst_d), writes=["cst"], dma_sem=s_cst)
        S.add("dve", lambda e: e.memset(onesm[:], 1.0 / D), writes=["onesm"])
        S.add("dve", lambda e: e.memset(onesb[:], 1.0), writes=["onesb"])
        S.add("dve", lambda e: e.memset(onesf[:], 1.0), writes=["onesf"])
        S.add("pool", lambda e: e.memset(Umat[:], 1.0), writes=["Umat"])
        S.add("pool", lambda e: e.memset(Lmat[:], 1.0), writes=["Lmat"])
        S.add("pool", lambda e: e.memset(identf[:], 1.0), writes=["identf"])
        S.add("pool", lambda e: e.affine_select(out=Umat[:], in_=Umat[:], pattern=[[1, 128]],
                                                 compare_op=ALU.is_ge, fill=0.0, base=0, channel_multiplier=-1),
              reads=["Umat"], writes=["Umat"])
        S.add("pool", lambda e: e.affine_select(out=Lmat[:], in_=Lmat[:], pattern=[[-1, 128]],
                                                 compare_op=ALU.is_gt, fill=0.0, base=0, channel_multiplier=1),
              reads=["Lmat"], writes=["Lmat"])
        S.add("pool", lambda e: e.affine_select(out=identf[:], in_=identf[:], pattern=[[1, 128]],
                                                 compare_op=ALU.is_gt, fill=0.0, base=0, channel_multiplier=-1),
              reads=["identf"], writes=["identf"])
        S.add("dve", lambda e: e.tensor_tensor(out=ident[:], in0=Umat[:], in1=identf[:], op=ALU.subtract),
              reads=["Umat", "identf"], writes=["ident"])

        def chk(tag):
            if getattr(cfg, "dbg", "") == tag:
                with nc.Block() as block:
                    S.emit(nc, block, engsem)
                raise _Stop()

        def load_x(tok0, slot0, n):
            for j in range(n // 512):
                a, b = tok0 + j * 512, slot0 + j * 512
                S.add("sp",
                      lambda e, a=a, b=b: e.dma_start(out=xT[:, :, b:b + 512], in_=xT_v[:, :, a:a + 512]),
                      writes=[("x", b // 512, c) for c in range(KC)], dma_sem=s_x[b // 512])

        def rstd_from_ps(bk, n, out_ap):
            S.add("dve", lambda e: e.tensor_scalar_add(out=rtmp[:, :n], in0=ps[bk][:, :n], scalar1=EPS),
                  reads=[("ps", bk)], writes=["rtmp"])
            S.add("act", lambda e: e.activation(out=rtmp[:, :n], in_=rtmp[:, :n], func=AF.Ln),
                  reads=["rtmp"], writes=["rtmp"])
            S.add("act", lambda e: e.activation(out=out_ap, in_=rtmp[:, :n], func=AF.Exp, scale=-0.5),
                  reads=["rtmp"], writes=["rstd"])

        def rms_norm(src_fn, src_keys, gcol, dst_fn, dst_keys, n=512):
            for c in range(KC):
                S.add("act", lambda e, c=c: e.activation(out=sq[:, c, :n], in_=src_fn(c), func=AF.Square),
                      reads=[src_keys[c]], writes=[("sq", c)])
            bk = psrot.next()
            S.add("pe", lambda e, bk=bk: [e.matmul(ps[bk][:, :n], onesm[:], sq[:, c, :n],
                                                   start=(c == 0), stop=(c == KC - 1)) for c in range(KC)][-1],
                  reads=["onesm"] + [("sq", c) for c in range(KC)], writes=[("ps", bk)])
            rstd_from_ps(bk, n, rstd[:, :n])
            for c in range(KC):
                S.add("dve", lambda e, c=c: e.scalar_tensor_tensor(out=dst_fn(c), in0=src_fn(c),
                                                                    scalar=cst[:, gcol + c:gcol + c + 1],
                                                                    in1=rstd[:, :n], op0=ALU.mult, op1=ALU.mult),
                      reads=[src_keys[c], "rstd", "cst"], writes=[dst_keys[c]])

        def hkeys(tok0):
            return [("hm", tok0 // 512, c) for c in range(KC)]

        def resid_add(bk, o, o0, scale):
            S.add("dve", lambda e: e.scalar_tensor_tensor(
                out=xT[:, o, o0:o0 + 512], in0=ps[bk][:], scalar=scale, in1=xT[:, o, o0:o0 + 512],
                op0=ALU.mult, op1=ALU.add),
                reads=[("ps", bk), ("x", o0 // 512, o)], writes=[("x", o0 // 512, o)])

        def ffn_phase(tiles, win_d, wout_d, gcol, post=None):
            al = Bump()
            aT_hi = al([128, FC - KC, T], BF16)
            NWI, NWO = 3, 2
            wi = [al([128, KC, 256], BF16) for _ in range(NWI)]
            wo = [al([128, FC, 128], BF16) for _ in range(NWO)]
            sg = [al([128, 512], F32) for _ in range(2)]
            s_wi = [sem(f"s_wi{i}_{gcol}") for i in range(NWI)]
            s_wo = [sem(f"s_wo{i}_{gcol}") for i in range(NWO)]
            wirot, worot, sgrot = Rot(range(NWI)), Rot(range(NWO)), Rot(range(2))
            H0 = NPRE
            A0 = NPRE + T
            win_v = win_d.rearrange("(c p) n -> p c n", p=128)
            wout_v = wout_d.rearrange("(f p) n -> p f n", p=128)

            def aTf(f, s):
                if f < KC:
                    return hTm[:, f, A0 + s * 512:A0 + (s + 1) * 512]
                return aT_hi[:, f - KC, s * 512:(s + 1) * 512]

            def norm_tile(tok0, slot0):
                if tok0 is not None:
                    load_x(tok0, slot0, T)
                for s in range(NS):
                    o0 = slot0 + s * 512
                    rms_norm(lambda c, o0=o0: xT[:, c, o0:o0 + 512], [("x", o0 // 512, c) for c in range(KC)], gcol,
                             lambda c, s=s: hTm[:, c, H0 + s * 512:H0 + (s + 1) * 512],
                             [("hf", s, c) for c in range(KC)])

            norm_tile(*tiles[0])
            for ti, (tok0, slot0) in enumerate(tiles):
                for f in range(FC):
                    w = wirot.next()
                    for half in range(2):
                        col = half * DFF + f * 128
                        S.add("pool", lambda e, w=w, half=half, col=col: e.dma_start(
                            out=wi[w][:, :, half * 128:(half + 1) * 128], in_=win_v[:, :, col:col + 128]),
                            writes=[("wi", w, half)], dma_sem=s_wi[w])
                    for s in range(NS):
                        bg, bu = psrot.next(), psrot.next()
                        for half, bk in ((0, bg), (1, bu)):
                            S.add("pe", lambda e, w=w, half=half, bk=bk, s=s: [
                                e.matmul(ps[bk][:], wi[w][:, c, half * 128:(half + 1) * 128],
                                         hTm[:, c, H0 + s * 512:H0 + (s + 1) * 512], start=(c == 0),
                                         stop=(c == KC - 1)) for c in range(KC)][-1],
                                reads=[("wi", w, half)] + [("hf", s, c) for c in range(KC)], writes=[("ps", bk)])
                        g = sgrot.next()
                        S.add("act", lambda e, g=g, bg=bg: e.activation(out=sg[g], in_=ps[bg][:], func=AF.Silu),
                              reads=[("ps", bg)], writes=[("sg", g)])
                        S.add("dve", lambda e, g=g, bu=bu, f=f, s=s: e.tensor_tensor(
                            out=aTf(f, s), in0=ps[bu][:], in1=sg[g], op=ALU.mult),
                            reads=[("ps", bu), ("sg", g)], writes=[("aT", f, s)])
                for o in range(KC):
                    w = worot.next()
                    S.add("pool", lambda e, w=w, o=o: e.dma_start(out=wo[w],
                                                                   in_=wout_v[:, :, o * 128:(o + 1) * 128]),
                          writes=[("wo", w)], dma_sem=s_wo[w])
                    for s in range(NS):
                        bk = psrot.next()
                        S.add("pe", lambda e, w=w, bk=bk, s=s: [
                            e.matmul(ps[bk][:], wo[w][:, f, :], aTf(f, s),
                                     start=(f == 0), stop=(f == FC - 1)) for f in range(FC)][-1],
                            reads=[("wo", w)] + [("aT", f, s) for f in range(FC)], writes=[("ps", bk)])
                        resid_add(bk, o, slot0 + s * 512, 0.5)
                    if o == 0 and ti + 1 < len(tiles) and tiles[ti + 1][1] != slot0:
                        norm_tile(*tiles[ti + 1])
                if post is not None:
                    post(tok0, slot0)
                if ti + 1 < len(tiles) and tiles[ti + 1][1] == slot0:
                    norm_tile(*tiles[ti + 1])

        def norm_mix_tile(tok0, slot0, n):
            for s in range(n // 512):
                o0, t0 = slot0 + s * 512, tok0 + s * 512
                rms_norm(lambda c, o0=o0: xT[:, c, o0:o0 + 512], [("x", o0 // 512, c) for c in range(KC)], C_GM,
                         lambda c, t0=t0: hTm[:, c, t0:t0 + 512], hkeys(t0))

        do_mix = upto >= 2
        tiles = []
        if do_mix:
            for pt in range(NPRE // T):
                tiles.append((pt * T, (pt % (NOWN // T)) * T))
        for t in range(NOWN // T):
            tiles.append((NPRE + t * T, t * T))

        def post1(tok0, slot0):
            if tok0 < NPRE:
                norm_mix_tile(tok0, slot0, T)

        ffn_phase(tiles, w1i_d, w1o_d, C_G1, post1)
        S.barrier()

        if do_mix:
            norm_mix_tile(NPRE, 0, NOWN)
            S.barrier()
            chk("c_mixnorm")
            s_wog = sem("s_wog")

            def wout_unit(yT, wog, fc0):
                S.add("pool", lambda e: e.dma_start(out=wog, in_=wom_v[:, fc0:fc0 + 2, :]),
                      writes=["wog"], dma_sem=s_wog)
                for o in range(KC):
                    for t in range(NTO):
                        bk = psrot.next()
                        S.add("pe", lambda e, bk=bk, o=o, t=t: [
                            e.matmul(ps[bk][:], wog[:, j, o * 128:(o + 1) * 128], yT[:, j, t * 512:(t + 1) * 512],
                                     start=(j == 0), stop=(j == 1)) for j in range(2)][-1],
                            reads=["wog", ("yT", 0), ("yT", 1)], writes=[("ps", bk)])
                        resid_add(bk, o, t * 512, 1.0)

            al = Bump()
            wdt = al([128, KC, 16], BF16)
            dtb = al([128, NB, 16], F32)
            la = al([128, NB, 16], F32)
            dte = al([128, NB, 16], F32)
            elc = al([128, NB, 16], F32)
            cd = al([128, NB, 16], F32)
            mark = al.off
            lc = al([128, NB, 16], F32)
            tA = al([128, 16], F32)
            tmpb = al([128, NB, 16], F32)
            tmpc = al([128, NB, 16], F32)
            al.off = mark
            wg = al([128, KC, 768], BF16)
            wog = al([128, 2, 1024], BF16)
            yT = sq[:].rearrange("p c t -> p (c t)").rearrange("p (i t) -> p i t", i=2)
            pre_sb = al([128, 4, 516], BF16)
            dg = al([128, 4, 4, 128], BF16)
            xc = al([128, 4, 512], BF16)
            CB = []
            for _b in range(2):
                CB.append(dict(
                    Btok=al([128, 128], BF16), xdt=al([128, 256], BF16), xdtd=al([128, 256], BF16),
                    ysk=al([128, 256], F32), cbm=al([128, 128], F32), Lm=al([128, 4, 128], F32),
                    MT=al([128, 4, 128], BF16), y1=al([128, 256], F32), y2=al([128, 256], F32),
                    ss=al([128, 4], F32), yn=al([128, 256], BF16)))
            zst = al([128, 4, 256], F32)
            Sf = al([128, 256], F32)
            S1 = al([128, 256], F32)
            Sbf = al([128, 256], BF16)
            s_wdt = sem("s_wdt")
            s_wg = sem("s_wg")
            NBW = NB * 16

            S.add("pool", lambda e: e.dma_start(out=wdt, in_=wim_v[:, :, O_DT:O_DT + 16]),
                  writes=["wdt"], dma_sem=s_wdt)
            bdt = psrot.next()
            for blk in range(NB):
                S.add("pe", lambda e, blk=blk: [
                    e.matmul(ps[bdt][:, blk * 16:(blk + 1) * 16], hTm[:, c, blk * 128:(blk + 1) * 128], wdt[:, c, :],
                             start=(c == 0), stop=(c == KC - 1)) for c in range(KC)][-1],
                    reads=["wdt"] + hkeys(blk * 128), writes=[("ps", bdt)])

            chk("c_dtraw")

            def v3(a):
                return a.rearrange("p (b h) -> p b h", h=16)

            def bc16(col):
                return cst[:, col:col + 16].unsqueeze(1).to_broadcast([128, NB, 16])

            S.add("dve", lambda e: e.tensor_tensor(out=tmpb, in0=v3(ps[bdt][:, :NBW]), in1=bc16(C_DTB), op=ALU.add),
                  reads=[("ps", bdt), "cst"], writes=["tmpb"])
            S.add("dve", lambda e: e.tensor_scalar_mul(out=tmpc, in0=tmpb, scalar1=-1.0),
                  reads=["tmpb"], writes=["tmpc"])
            S.add("dve", lambda e: e.tensor_tensor(out=tmpc, in0=tmpc, in1=tmpb, op=ALU.min),
                  reads=["tmpb", "tmpc"], writes=["tmpc"])
            S.add("act", lambda e: e.activation(out=tmpc, in_=tmpc, func=AF.Exp),
                  reads=["tmpc"], writes=["tmpc"])
            S.add("dve", lambda e: e.tensor_scalar_add(out=tmpc, in0=tmpc, scalar1=1.0),
                  reads=["tmpc"], writes=["tmpc"])
            S.add("act", lambda e: e.activation(out=tmpc, in_=tmpc, func=AF.Ln),
                  reads=["tmpc"], writes=["tmpc"])
            S.add("dve", lambda e: e.scalar_tensor_tensor(out=dtb, in0=tmpb, scalar=0.0, in1=tmpc,
                                                           op0=ALU.max, op1=ALU.add),
                  reads=["tmpb", "tmpc"], writes=["dtb"])
            chk("c_softplus")
            S.add("act", lambda e: e.activation(out=tA, in_=cst[:, C_ALOG:C_ALOG + 16], func=AF.Exp),
                  reads=["cst"], writes=["tA"])
            S.add("dve", lambda e: e.scalar_tensor_tensor(
                out=la, in0=dtb, scalar=-1.0, in1=tA.unsqueeze(1).to_broadcast([128, NB, 16]),
                op0=ALU.mult, op1=ALU.mult), reads=["dtb", "tA"], writes=["la"])
            chk("c_la")
            blc, btot = psrot.next(), psrot.next()
            for blk in range(NB):
                S.add("pe", lambda e, blk=blk: e.matmul(ps[blc][:, blk * 16:(blk + 1) * 16], Umat[:], la[:, blk, :],
                                                         start=True, stop=True),
                      reads=["Umat", "la"], writes=[("ps", blc)])
                S.add("pe", lambda e, blk=blk: e.matmul(ps[btot][:, blk * 16:(blk + 1) * 16], onesf[:], la[:, blk, :],
                                                         start=True, stop=True),
                      reads=["onesf", "la"], writes=[("ps", btot)])
            chk("c_lcmm")
            S.add("dve", lambda e: e.tensor_copy(out=lc, in_=v3(ps[blc][:, :NBW])),
                  reads=[("ps", blc)], writes=["lc"])
            chk("c_lccopy")
            S.add("act", lambda e: e.activation(out=elc, in_=lc, func=AF.Exp),
                  reads=["lc"], writes=["elc"])
            chk("c_elc")
            S.add("act", lambda e: e.activation(out=cd.rearrange("p b h -> p (b h)"), in_=ps[btot][:, :NBW],
                                                func=AF.Exp),
                  reads=[("ps", btot)], writes=["cd"])
            chk("c_cd")
            S.add("dve", lambda e: e.tensor_tensor(out=dte, in0=v3(ps[btot][:, :NBW]), in1=lc, op=ALU.subtract),
                  reads=[("ps", btot), "lc"], writes=["dte"])
            chk("c_dtesub")
            S.add("act", lambda e: e.activation(out=dte, in_=dte, func=AF.Exp), reads=["dte"], writes=["dte"])

            S.barrier()
            chk("c_tables")

            def bc64(a):
                return a.unsqueeze(2).to_broadcast([128, 4, 64])

            def h4(a):
                return a.rearrange("p (h d) -> p h d", h=4)

            class Seg:
                def __init__(self):
                    self.ops = []

                def add(self, eng, fn, reads=(), writes=()):
                    self.ops.append((eng, fn, list(reads), list(writes)))

            def flush(segs):
                n = max([len(sg.ops) for sg in segs] + [0])
                for i in range(n):
                    for sg in segs:
                        if i < len(sg.ops):
                            eng, fn, r, w = sg.ops[i]
                            S.add(eng, fn, reads=r, writes=w)

            def seg_A(g, j, bi, wgk):
                blk = j * 4 + bi
                own = blk >= NBP
                b = blk % 2
                B_ = CB[b]
                cs = slice(bi * 128, (bi + 1) * 128)
                hs = slice(4 * g, 4 * g + 4)
                sg = Seg()
                btp = 2
                tp = ps[btp][:].bitcast(BF16)
                sg.add("pe", lambda e: [e.transpose(tp[:, i * 128:(i + 1) * 128], xc[:, i, cs], ident[:])
                                        for i in range(3)][-1],
                       reads=["ident", ("xc", 0), ("xc", 1), ("xc", 2)], writes=[("ps", btp)])
                sg.add("act", lambda e: e.copy(out=B_["Btok"], in_=tp[:, 256:384]),
                       reads=[("ps", btp)], writes=[("Btok", b)])
                sg.add("dve", lambda e: e.tensor_tensor(out=h4(B_["xdt"]), in0=h4(tp[:, 0:256]),
                                                        in1=bc64(dtb[:, blk, hs]), op=ALU.mult),
                       reads=[("ps", btp), "dtb"], writes=[("xdt", b)])
                sg.add("pool", lambda e: e.tensor_tensor(out=h4(B_["xdtd"]), in0=h4(B_["xdt"]),
                                                        in1=bc64(dte[:, blk, hs]), op=ALU.mult),
                       reads=[("xdt", b), "dte"], writes=[("xdtd", b)])
                if own:
                    sg.add("dve", lambda e: e.tensor_tensor(
                        out=h4(B_["ysk"]), in0=h4(tp[:, 0:256]),
                        in1=bc64(cst[:, C_DSKIP + 4 * g:C_DSKIP + 4 * g + 4]), op=ALU.mult),
                        reads=[("ps", btp), "cst"], writes=[("ysk", b)])
                bst = b
                sg.add("pe", lambda e: e.matmul(ps[bst][:, 0:256], B_["Btok"], B_["xdtd"], start=True, stop=True),
                       reads=[("Btok", b), ("xdtd", b)], writes=[("ps", bst)])
                if own:
                    bcb = 3
                    sg.add("pe", lambda e: e.matmul(ps[bcb][:, 0:128], xc[:, 2, cs], xc[:, 3, cs],
                                                    start=True, stop=True),
                           reads=[("xc", 2), ("xc", 3)], writes=[("ps", bcb)])
                    sg.add("dve", lambda e: e.tensor_tensor(out=B_["cbm"], in0=ps[bcb][:, 0:128], in1=Umat[:],
                                                            op=ALU.mult),
                           reads=[("ps", bcb), "Umat"], writes=[("cbm", b)])
                    sg.add("pool", lambda e: e.tensor_tensor(
                        out=B_["Lm"], in0=Lmat[:].unsqueeze(1).to_broadcast([128, 4, 128]),
                        in1=la[:, blk, hs].unsqueeze(2).to_broadcast([128, 4, 128]), op=ALU.mult),
                        reads=["Lmat", "la"], writes=[("Lm", b, h) for h in range(4)])
                    bsg = 4
                    sg.add("pe", lambda e: [e.matmul(ps[bsg][:, h * 128:(h + 1) * 128], B_["Lm"][:, h, :], Umat[:],
                                                     start=True, stop=True) for h in range(4)][-1],
                           reads=["Umat"] + [("Lm", b, h) for h in range(4)], writes=[("ps", bsg)])
                    sg.add("act", lambda e: e.activation(out=B_["Lm"].rearrange("p h s -> p (h s)"), in_=ps[bsg][:],
                                                         func=AF.Exp),
                           reads=[("ps", bsg)], writes=[("Lm", b, h) for h in range(4)])
                    sg.add("pool", lambda e: e.tensor_tensor(
                        out=B_["MT"], in0=B_["Lm"], in1=B_["cbm"].unsqueeze(1).to_broadcast([128, 4, 128]),
                        op=ALU.mult),
                        reads=[("Lm", b, h) for h in range(4)] + [("cbm", b)], writes=[("MT", b)])
                return sg, bst

            def seg_B(g, j, bi, wgk, bst):
                blk = j * 4 + bi
                own = blk >= NBP
                b = blk % 2
                B_ = CB[b]
                cs = slice(bi * 128, (bi + 1) * 128)
                hs = slice(4 * g, 4 * g + 4)
                sg = Seg()
                if own:
                    by = 5

                    def fny(e):
                        for h in range(4):
                            e.matmul(ps[by][:, h * 64:(h + 1) * 64], B_["MT"][:, h, :],
                                     B_["xdt"][:, h * 64:(h + 1) * 64], start=True, stop=True)
                        return e.matmul(ps[by][:, 256:512], xc[:, 3, cs], Sbf, start=True, stop=True)
                    sg.add("pe", fny, reads=[("MT", b), ("xdt", b), ("xc", 3), "Sbf"], writes=[("ps", by)])
                if blk < NB - 1:
                    sg.add("pool", lambda e: e.tensor_tensor(out=h4(S1), in0=h4(Sf), in1=bc64(cd[:, blk, hs]),
                                                            op=ALU.mult),
                           reads=["Sf", "cd"], writes=["S1"])
                    sg.add("dve", lambda e: e.tensor_tensor(out=Sf, in0=ps[bst][:, 0:256], in1=S1, op=ALU.add),
                           reads=[("ps", bst), "S1"], writes=["Sf"])
                    if blk == NBP - 1:
                        sg.add("dve", lambda e: e.tensor_scalar_mul(out=Sf, in0=Sf, scalar1=cst[:, C_FLAG:C_FLAG + 1]),
                               reads=["Sf", "cst"], writes=["Sf"])
                    sg.add("act", lambda e: e.copy(out=Sbf, in_=Sf), reads=["Sf"], writes=["Sbf"])
                if own:
                    sg.add("dve", lambda e: e.tensor_tensor(out=h4(B_["y1"]), in0=h4(ps[by][:, 256:512]),
                                                            in1=bc64(elc[:, blk, hs]), op=ALU.mult),
                           reads=[("ps", by), "elc"], writes=[("y1", b)])
                    sg.add("dve", lambda e: e.tensor_tensor(out=B_["y2"], in0=ps[by][:, 0:256], in1=B_["y1"],
                                                            op=ALU.add),
                           reads=[("ps", by), ("y1", b)], writes=[("y2", b)])
                    sg.add("dve", lambda e: e.tensor_tensor(out=B_["y2"], in0=B_["y2"], in1=B_["ysk"], op=ALU.add),
                           reads=[("y2", b), ("ysk", b)], writes=[("y2", b)])
                    sg.add("dve", lambda e: e.tensor_tensor(out=B_["y2"], in0=B_["y2"], in1=zst[:, bi, :], op=ALU.mult),
                           reads=[("y2", b), ("zst", bi)], writes=[("y2", b)])
                    sg.add("pool", lambda e: e.tensor_tensor(out=B_["y1"], in0=B_["y2"], in1=B_["y2"], op=ALU.mult),
                           reads=[("y2", b)], writes=[("y1", b)])
                    ss = B_["ss"]
                    sg.add("dve", lambda e: e.reduce_sum(out=ss[:, 0:1], in_=B_["y1"], axis=AX.X),
                           reads=[("y1", b)], writes=[("ss", b)])
                    sg.add("dve", lambda e: e.tensor_scalar(out=ss[:, 1:2], in0=ss[:, 0:1], scalar1=1.0 / 256,
                                                            scalar2=EPS, op0=ALU.mult, op1=ALU.add),
                           reads=[("ss", b)], writes=[("ss", b)])
                    sg.add("act", lambda e: e.activation(out=ss[:, 2:3], in_=ss[:, 1:2], func=AF.Ln),
                           reads=[("ss", b)], writes=[("ss", b)])
                    sg.add("act", lambda e: e.activation(out=ss[:, 3:4], in_=ss[:, 2:3], func=AF.Exp, scale=-0.5),
                           reads=[("ss", b)], writes=[("ss", b)])
                    sg.add("dve", lambda e: e.tensor_scalar_mul(out=B_["yn"], in0=B_["y2"], scalar1=ss[:, 3:4]),
                           reads=[("y2", b), ("ss", b)], writes=[("yn", b)])
                return sg

            def seg_C(g, j, bi):
                blk = j * 4 + bi
                own = blk >= NBP
                b = blk % 2
                B_ = CB[b]
                sg = Seg()
                if own:
                    ob = blk - NBP
                    bt2 = 7
                    tp2 = ps[bt2][:].bitcast(BF16)
                    sg.add("pe", lambda e: [e.transpose(tp2[:, i * 128:(i + 1) * 128],
                                                        B_["yn"][:, i * 128:(i + 1) * 128], ident[:])
                                            for i in range(2)][-1],
                           reads=["ident", ("yn", b)], writes=[("ps", bt2)])
                    for i in range(2):
                        sg.add("dve", lambda e, i=i: e.tensor_scalar_mul(
                            out=yT[:, i, ob * 128:(ob + 1) * 128], in0=tp2[:, i * 128:(i + 1) * 128],
                            scalar1=cst[:, C_SSDW + 2 * g + i:C_SSDW + 2 * g + i + 1]),
                            reads=[("ps", bt2), "cst"], writes=[("yT", i)])
                return sg

            def load_wg(gg):
                for (dst0, src0, n) in ((0, O_XBC + gg * 256, 256), (256, O_XBC + 1024 + gg * 128, 128),
                                        (384, O_XBC + 1536 + gg * 128, 128), (512, O_Z + gg * 256, 256)):
                    S.add("pool", lambda e, dst0=dst0, src0=src0, n=n: e.dma_start(
                        out=wg[:, :, dst0:dst0 + n], in_=wim_v[:, :, src0:src0 + n]),
                        writes=[("wg", dst0)], dma_sem=s_wg)

            load_wg(0)
            for g in range(4):
                wgk = [("wg", 0), ("wg", 256), ("wg", 384), ("wg", 512)]
                chunk16 = [2 * g, 2 * g + 1, 8 + g, 12 + g]
                S.add("dve", lambda e: e.memset(Sf, 0.0), writes=["Sf"])
                S.add("dve", lambda e: e.memset(Sbf, 0.0), writes=["Sbf"])
                S.add("dve", lambda e: e.memset(pre_sb[:, :, 0:3], 0.0), writes=[("pre", cc) for cc in range(4)])
                for cc in range(4):
                    for tap in range(4):
                        wcol = C_CONVW + chunk16[cc] * 4 + tap
                        S.add("dve", lambda e, cc=cc, tap=tap, wcol=wcol: e.tensor_scalar_mul(
                            out=dg[:, cc, tap, :], in0=ident[:], scalar1=cst[:, wcol:wcol + 1]),
                            reads=["ident", "cst"], writes=[("dg", cc)])
                def conv_stage(j, g=g, wgk=wgk, chunk16=chunk16):
                    t0 = j * 512
                    lastpre = NPRE // 512 - 1
                    need_pre = [True, True, True, j >= lastpre]
                    need_cv = [True, True, True, j > lastpre]
                    bks = [psrot.next() for _ in range(4)]
                    bcs = [psrot.next() for _ in range(4)]
                    for cc in range(4):
                        if not need_pre[cc]:
                            continue
                        bk = bks[cc]
                        S.add("pe", lambda e, bk=bk, cc=cc, t0=t0: [
                            e.matmul(ps[bk][:], wg[:, c, cc * 128:(cc + 1) * 128], hTm[:, c, t0:t0 + 512],
                                     start=(c == 0), stop=(c == KC - 1)) for c in range(KC)][-1],
                            reads=wgk + hkeys(t0), writes=[("ps", bk)])
                    for cc in range(4):
                        if not need_pre[cc]:
                            continue
                        bk = bks[cc]
                        if j > 0 and (cc < 3 or j > lastpre):
                            S.add("act", lambda e, cc=cc: e.copy(out=pre_sb[:, cc, 0:3], in_=pre_sb[:, cc, 512:515]),
                                  reads=[("pre", cc)], writes=[("pre", cc)])
                            if j == NPRE // 512:
                                S.add("dve", lambda e, cc=cc: e.tensor_scalar_mul(
                                    out=pre_sb[:, cc, 0:3], in0=pre_sb[:, cc, 0:3], scalar1=cst[:, C_FLAG:C_FLAG + 1]),
                                    reads=[("pre", cc), "cst"], writes=[("pre", cc)])
                        S.add("act", lambda e, bk=bk, cc=cc: e.copy(out=pre_sb[:, cc, 3:515], in_=ps[bk][:]),
                              reads=[("ps", bk), ("pre", cc)], writes=[("pre", cc)])
                    for cc in range(4):
                        if not need_cv[cc]:
                            continue
                        bc_ = bcs[cc]
                        S.add("pe", lambda e, bc_=bc_, cc=cc: [
                            e.matmul(ps[bc_][:], dg[:, cc, tap, :], pre_sb[:, cc, tap:tap + 512],
                                     start=(tap == 0), stop=(tap == 3)) for tap in range(4)][-1],
                            reads=[("dg", cc), ("pre", cc)], writes=[("ps", bc_)])
                    for cc in range(4):
                        if not need_cv[cc]:
                            continue
                        bc_ = bcs[cc]
                        bcol = C_CONVB + chunk16[cc]
                        S.add("act", lambda e, cc=cc, bcol=bcol, bc_=bc_: e.activation(
                            out=xc[:, cc, :], in_=ps[bc_][:], func=AF.Silu, bias=cst[:, bcol:bcol + 1]),
                            reads=[("ps", bc_), "cst"], writes=[("xc", cc)])
                    if j * 4 >= NBP:
                        for hb in range(2):
                            bzz = psrot.next()
                            for q2 in range(2):
                                bi_ = hb * 2 + q2
                                blk_ = j * 4 + bi_
                                S.add("pe", lambda e, bzz=bzz, q2=q2, blk_=blk_: [
                                    e.matmul(ps[bzz][:, q2 * 256:(q2 + 1) * 256],
                                             hTm[:, c, blk_ * 128:(blk_ + 1) * 128], wg[:, c, 512:768],
                                             start=(c == 0), stop=(c == KC - 1)) for c in range(KC)][-1],
                                    reads=wgk + hkeys(blk_ * 128), writes=[("ps", bzz)])
                            S.add("act", lambda e, bzz=bzz, hb=hb: e.activation(
                                out=zst[:, hb * 2:hb * 2 + 2, :].rearrange("p a b -> p (a b)"), in_=ps[bzz][:],
                                func=AF.Silu),
                                reads=[("ps", bzz)], writes=[("zst", hb * 2), ("zst", hb * 2 + 1)])

                def split(sg, n):
                    a, b2 = Seg(), Seg()
                    a.ops, b2.ops = sg.ops[:n], sg.ops[n:]
                    return a, b2

                conv_stage(0)
                for j in range(NTT):
                    if g == 0 and j == 0:
                        chk("c_conv")
                    SA, SB, SC = [], [], []
                    for bi in range(4):
                        sgA, bst = seg_A(g, j, bi, wgk)
                        own = (j * 4 + bi) >= NBP
                        SA.append(split(sgA, 6 if own else 5))
                        sgB = seg_B(g, j, bi, wgk, bst)
                        nb1 = len(sgB.ops) - 6 if own else len(sgB.ops)
                        SB.append(split(sgB, nb1))
                        SC.append(seg_C(g, j, bi))

                    def do_round(r):
                        segs = []
                        if r < 4:
                            segs += list(SA[r])
                        if 0 <= r - 1 < 4:
                            segs.append(SB[r - 1][0])
                        if 0 <= r - 2 < 4:
                            segs.append(SB[r - 2][1])
                        if 0 <= r - 3 < 4:
                            segs.append(SC[r - 3])
                        flush(segs)
                    for r in range(5):
                        do_round(r)
                    if j + 1 < NTT:
                        conv_stage(j + 1)
                    for r in range(5, 7):
                        do_round(r)
                if g + 1 < 4:
                    load_wg(g + 1)
                wout_unit(yT, wog, 2 * g)
            S.barrier()

            if getattr(cfg, "dbg", "") == "ssd":
                with nc.Block() as block:
                    S.emit(nc, block, engsem)
                return nc
            al = Bump()
            whs = [al([128, KC, 384], BF16) for _ in range(2)]
            KT = al([128, NTOK], BF16)
            QT = al([128, NOWN], BF16)
            V = al([128, NB, 130], BF16)
            btf = al([128, 3, 128], F32)
            btm = al([128, 3, 128], BF16)
            PTm = [al([128, 2, 512], BF16) for _ in range(2)]
            PT = [[PTm[kk][:, c, :] for kk in range(2)] for c in range(2)]
            wog2 = al([128, 2, 1024], BF16)
            yT2 = al([128, 2, NOWN], BF16)
            lt = al([128, 64], F32)
            lv = al([128, 8], F32)
            rc = al([128, 4, 2], F32)
            rs = al([128, 16], F32)
            t2 = al([128, 4, 128], F32)
            ov = al([128, 4, 128], F32)
            osq = al([128, 4, 128], F32)
            on = al([128, 4, 128], BF16)
            pending2 = []
            s_whs = [sem("s_wh0"), sem("s_wh1")]
            s_bt = sem("s_bt")
            if getattr(cfg, "dbg", "") == "preamble0":
                with nc.Block() as block:
                    S.emit(nc, block, engsem)
                return nc
            for i, (ca, cb) in enumerate(((C_LQ1, C_LK1), (C_LQ2, C_LK2))):
                S.add("dve", lambda e, ca=ca, cb=cb: e.tensor_tensor(out=lt, in0=cst[:, ca:ca + 64],
                                                                      in1=cst[:, cb:cb + 64], op=ALU.mult),
                      reads=["cst"], writes=["lt"])
                S.add("dve", lambda e, i=i: e.reduce_sum(out=lv[:, i:i + 1], in_=lt, axis=AX.X),
                      reads=["lt"], writes=[("lv", i)])
            S.add("act", lambda e: e.activation(out=lv[:, 2:4], in_=lv[:, 0:2], func=AF.Exp),
                  reads=[("lv", 0), ("lv", 1)], writes=[("lv", 2)])
            S.add("dve", lambda e: e.tensor_tensor(out=lv[:, 4:5], in0=lv[:, 3:4], in1=lv[:, 2:3], op=ALU.subtract),
                  reads=[("lv", 2)], writes=[("lv", 4)])
            S.add("dve", lambda e: e.tensor_scalar_add(out=lv[:, 5:6], in0=lv[:, 4:5], scalar1=-LAMBDA_INIT),
                  reads=[("lv", 4)], writes=[("lv", 5)])
            if getattr(cfg, "dbg", "") == "preamble1":
                with nc.Block() as block:
                    S.emit(nc, block, engsem)
                return nc
            S.add("dve", lambda e: e.memset(V[:, :, 128:130], 1.0), writes=["Vones"])
            bias_v = bias_d.rearrange("p (h t q) -> p h t q", h=8, t=3)
            srot = Rot([0, 1])
            prot = Rot([0, 1, 2, 3, 7])
            OB = [4, 5, 6]

            def po(c, r):
                k = c * 4 + r
                return ps[OB[k // 3]][:, (k % 3) * 130:(k % 3) * 130 + 130]

            def pokey(c, r):
                return ("ps", OB[(c * 4 + r) // 3])

            NH_ = int(getattr(cfg, "nheads", 8))

            def load_wh(hh):
                for (dst0, src0) in ((0, O_Q + hh * 128), (128, O_K + hh * 128), (256, O_V + hh * 128)):
                    S.add("pool", lambda e, dst0=dst0, src0=src0, hh=hh: e.dma_start(
                        out=whs[hh % 2][:, :, dst0:dst0 + 128], in_=wim_v[:, :, src0:src0 + 128]),
                        writes=[("wh", hh % 2, dst0)], dma_sem=s_whs[hh % 2])

            if NH_ > 0:
                load_wh(0)
            for h in range(NH_):
                wh = whs[h % 2]
                whk = h % 2
                if h + 1 < NH_:
                    load_wh(h + 1)
                S.add("sp", lambda e, h=h: e.dma_start(out=btf, in_=bias_v[:, h, :, :]),
                      writes=["btf"], dma_sem=s_bt)
                for t in range(3):
                    col = (C_FAROWN if t < 2 else C_FARPRE) + h
                    S.add("dve", lambda e, t=t, col=col: e.tensor_scalar(
                        out=btm[:, t, :], in0=btf[:, t, :], scalar1=cst[:, col:col + 1], scalar2=8.0,
                        op0=ALU.subtract, op1=ALU.mult),
                        reads=["btf", "cst"], writes=[("btm", t)])
                evrot = Rot(["act", "dve"])

                def evac(dst, src):
                    eng = evrot.next()
                    if eng == "act":
                        S.add("act", lambda e: e.copy(out=dst, in_=src[1]), reads=[("ps", src[0])], writes=["kqv"])
                    else:
                        S.add("dve", lambda e: e.tensor_copy(out=dst, in_=src[1]), reads=[("ps", src[0])],
                              writes=["kqv"])

                for j in range(NTT):
                    bk = prot.next()
                    S.add("pe", lambda e, bk=bk, j=j, wh=wh: [
                        e.matmul(ps[bk][:], wh[:, c, 128:256], hTm[:, c, j * 512:(j + 1) * 512],
                                 start=(c == 0), stop=(c == KC - 1)) for c in range(KC)][-1],
                        reads=[("wh", whk, 128)] + hkeys(j * 512), writes=[("ps", bk)])
                    evac(KT[:, j * 512:(j + 1) * 512], (bk, ps[bk][:]))
                for j in range(NTO):
                    bk = prot.next()
                    S.add("pe", lambda e, bk=bk, j=j, wh=wh: [
                        e.matmul(ps[bk][:], wh[:, c, 0:128], hTm[:, c, NPRE + j * 512:NPRE + (j + 1) * 512],
                                 start=(c == 0), stop=(c == KC - 1)) for c in range(KC)][-1],
                        reads=[("wh", whk, 0)] + hkeys(NPRE + j * 512), writes=[("ps", bk)])
                    evac(QT[:, j * 512:(j + 1) * 512], (bk, ps[bk][:]))
                for j in range(NTT):
                    bk = prot.next()
                    for bi in range(4):
                        blk = j * 4 + bi
                        S.add("pe", lambda e, bk=bk, bi=bi, blk=blk, wh=wh: [
                            e.matmul(ps[bk][:, bi * 128:(bi + 1) * 128], hTm[:, c, blk * 128:(blk + 1) * 128],
                                     wh[:, c, 256:384], start=(c == 0), stop=(c == KC - 1))
                            for c in range(KC)][-1],
                            reads=[("wh", whk, 256)] + hkeys(blk * 128), writes=[("ps", bk)])
                    S.add("dve", lambda e, bk=bk, j=j: e.tensor_copy(
                        out=V[:, j * 4:(j + 1) * 4, 0:128], in_=ps[bk][:].rearrange("p (b e) -> p b e", b=4)),
                        reads=[("ps", bk)], writes=["kqv"])

                for qc in range(NTO):
                    kbs = [(kb, True) for kb in range(NBP)] + [(NBP + jj, False) for jj in range(4 * qc + 4)]

                    def emit_S(ki, kb, is_pre, qc=qc, h=h):
                        jj = kb - NBP
                        r0 = 0 if (is_pre or jj < 4 * qc) else jj - 4 * qc
                        qa = r0 * 128
                        pair = srot.next()
                        for c in range(2):
                            sbk = 2 * pair + c
                            spec = []
                            if is_pre:
                                if kb == NBP - 1 and qc == 0:
                                    spec.append((0, 2))
                            else:
                                r = jj - 4 * qc
                                if 0 <= r < 4:
                                    spec.append((r, 0))
                                if 0 <= r + 1 < 4:
                                    spec.append((r + 1, 1))

                            def fn(e, c=c, sbk=sbk, kb=kb, qa=qa, qc=qc, spec=spec):
                                ins = e.matmul(ps[sbk][:, qa:512], KT[c * 64:(c + 1) * 64, kb * 128:(kb + 1) * 128],
                                               QT[c * 64:(c + 1) * 64, qc * 512 + qa:qc * 512 + 512],
                                               start=True, stop=(len(spec) == 0))
                                for i, (r, t) in enumerate(spec):
                                    ins = e.matmul(ps[sbk][:, r * 128:(r + 1) * 128], ident[:], btm[:, t, :],
                                                   start=False, stop=(i == len(spec) - 1))
                                return ins
                            S.add("pe", fn, reads=["kqv", "ident"] + [("btm", t) for (_, t) in spec],
                                  writes=[("ps", sbk)])
                            pt = PT[c][ki % 2]
                            bcol = (C_FARPRE if is_pre else C_FAROWN) + h
                            if qa > 0:
                                S.add("act", lambda e, pt=pt, sbk=sbk, qa=qa, bcol=bcol: e.activation(
                                    out=pt[:, qa:512], in_=ps[sbk][:, qa:512], func=AF.Exp,
                                    bias=cst[:, bcol:bcol + 1], scale=0.125),
                                    reads=[("ps", sbk), "cst"], writes=[("PT", c, ki % 2)])
                        if qa == 0:
                            S.add("act", lambda e, pair=pair, kk=ki % 2, bcol=bcol: e.activation(
                                out=PTm[kk].rearrange("p c q -> p (c q)"), in_=psp[pair][:, :], func=AF.Exp,
                                bias=cst[:, bcol:bcol + 1], scale=0.125),
                                reads=[("ps", 2 * pair), ("ps", 2 * pair + 1), "cst"],
                                writes=[("PT", 0, ki % 2), ("PT", 1, ki % 2)])

                    def emit_PV(ki, kb, is_pre, qc=qc):
                        jj = kb - NBP
                        r0 = 0 if (is_pre or jj < 4 * qc) else jj - 4 * qc
                        for c in range(2):
                            pt = PT[c][ki % 2]

                            def fn2(e, c=c, pt=pt, kb=kb, r0=r0, ki=ki, jj=jj, qc=qc, is_pre=is_pre):
                                ins = None
                                for r in range(r0, 4):
                                    last = (not is_pre) and (jj == 4 * qc + r)
                                    ins = e.matmul(po(c, r), pt[:, r * 128:(r + 1) * 128], V[:, kb, :],
                                                   start=(ki == 0 and (c * 4 + r) % 3 == 0), stop=last,
                                                   skip_group_check=True)
                                return ins
                            S.add("pe", fn2, reads=[("PT", c, ki % 2), "kqv", "Vones"],
                                  writes=sorted({pokey(c, r) for r in range(r0, 4)}))

                    for ki, (kb, is_pre) in enumerate(kbs):
                        emit_S(ki, kb, is_pre)
                        if ki > 0:
                            emit_PV(ki - 1, *kbs[ki - 1])
                        if ki == 3:
                            while pending2:
                                pending2.pop(0)()
                    emit_PV(len(kbs) - 1, *kbs[-1])
                    for r in range(4):
                        S.add("dve", lambda e, r=r: e.reciprocal(out=rc[:, r, 0:1], in_=po(0, r)[:, 128:129]),
                              reads=[pokey(0, r)], writes=[("rc", r)])
                        S.add("dve", lambda e, r=r: e.reciprocal(out=rc[:, r, 1:2], in_=po(1, r)[:, 128:129]),
                              reads=[pokey(1, r)], writes=[("rc", r)])
                        S.add("dve", lambda e, r=r: e.tensor_scalar(
                            out=t2[:, r, :], in0=po(1, r)[:, 0:128], scalar1=rc[:, r, 1:2], scalar2=lv[:, 5:6],
                            op0=ALU.mult, op1=ALU.mult),
                            reads=[pokey(1, r), ("rc", r), ("lv", 5)], writes=[("t2", r)])
                        S.add("dve", lambda e, r=r: e.scalar_tensor_tensor(
                            out=ov[:, r, :], in0=po(0, r)[:, 0:128], scalar=rc[:, r, 0:1], in1=t2[:, r, :],
                            op0=ALU.mult, op1=ALU.add),
                            reads=[pokey(0, r), ("rc", r), ("t2", r)], writes=["ov"])

                    def step2(qc=qc, h=h):
                        S.add("dve", lambda e: e.tensor_tensor(out=osq, in0=ov, in1=ov, op=ALU.mult),
                              reads=["ov"], writes=["osq"])
                        S.add("dve", lambda e: e.reduce_sum(out=rs[:, 0:4], in_=osq, axis=AX.X),
                              reads=["osq"], writes=["rs"])
                        S.add("dve", lambda e: e.tensor_scalar(out=rs[:, 4:8], in0=rs[:, 0:4], scalar1=1.0 / 128,
                                                                scalar2=EPS, op0=ALU.mult, op1=ALU.add),
                              reads=["rs"], writes=["rs"])
                        S.add("act", lambda e: e.activation(out=rs[:, 8:12], in_=rs[:, 4:8], func=AF.Ln),
                              reads=["rs"], writes=["rs"])
                        S.add("act", lambda e: e.activation(out=rs[:, 12:16], in_=rs[:, 8:12], func=AF.Exp, scale=-0.5),
                              reads=["rs"], writes=["rs"])
                        S.add("dve", lambda e: e.tensor_tensor(
                            out=on, in0=ov, in1=rs[:, 12:16].unsqueeze(2).to_broadcast([128, 4, 128]), op=ALU.mult),
                            reads=["ov", "rs"], writes=["on"])
                        tpb = ps[7][:].bitcast(BF16)
                        S.add("pe", lambda e: [e.transpose(tpb[:, r * 128:(r + 1) * 128], on[:, r, :], ident[:])
                                               for r in range(4)][-1],
                              reads=["on", "ident"], writes=[("ps", 7)])
                        S.add("dve", lambda e: e.tensor_scalar(
                            out=yT2[:, h % 2, qc * 512:(qc + 1) * 512], in0=tpb[:, 0:512],
                            scalar1=cst[:, C_SUBLN:C_SUBLN + 1], scalar2=1.0 - LAMBDA_INIT,
                            op0=ALU.mult, op1=ALU.mult),
                            reads=[("ps", 7), "cst"], writes=[("yT", h % 2)])
                    pending2.append(step2)
                while pending2:
                    pending2.pop(0)()
                if h % 2 == 1 and getattr(cfg, "dbg", "") != "noattnwout":
                    wout_unit(yT2, wog2, 8 + h - 1)
            S.barrier()

        if getattr(cfg, "dbg", "") == "postattn":
            with nc.Block() as block:
                S.emit(nc, block, engsem)
            return nc
        if upto >= 3:
            al = Bump()
            wq = hTm[:, :, 0:1024]
            woc = hTm[:, :, 1024:2048]
            KmT = al([128, 8, MEM], BF16)
            Vm = al([128, 2, 1024], BF16)
            mark3 = al.off
            mT = al([128, KC, MEM], F32)
            mhT = al([128, KC, MEM], BF16)
            wkv = al([128, KC, 1024], BF16)
            s_cw = sem("s_cw")
            s_kv = sem("s_kv")
            s_m = sem("s_m")
            wcq_v = wcq_d.rearrange("(c p) n -> p c n", p=128)
            wck_v = wck_d.rearrange("(c p) n -> p c n", p=128)
            wcv_v = wcv_d.rearrange("(c p) n -> p c n", p=128)
            wco_v = wco_d.rearrange("(c p) n -> p c n", p=128)
            for hh in range(2):
                S.add("pool", lambda e, hh=hh: e.dma_start(out=wq[:, :, hh * 512:(hh + 1) * 512],
                                                            in_=wcq_v[:, :, hh * 512:(hh + 1) * 512]),
                      writes=["wq"], dma_sem=s_cw)
                S.add("pool", lambda e, hh=hh: e.dma_start(out=woc[:, :, hh * 512:(hh + 1) * 512],
                                                            in_=wco_v[:, :, hh * 512:(hh + 1) * 512]),
                      writes=["woc"], dma_sem=s_cw)
            S.add("sp", lambda e: e.dma_start(out=mT, in_=memT_d.rearrange("(c p) t -> p c t", p=128)),
                  writes=[("mT", c) for c in range(KC)], dma_sem=s_m)
            HC0 = 2048
            for t in range(NTO):
                o0 = t * 512
                rms_norm(lambda c, o0=o0: xT[:, c, o0:o0 + 512], [("x", t, c) for c in range(KC)], C_GC,
                         lambda c, o0=o0: hTm[:, c, HC0 + o0:HC0 + o0 + 512], [("hcT", t, c) for c in range(KC)])
            rms_norm(lambda c: mT[:, c, :], [("mT", c) for c in range(KC)], C_GMEM,
                     lambda c: mhT[:, c, :], [("mhT", c) for c in range(KC)], n=MEM)
            mk = [("mhT", c) for c in range(KC)]
            for hh in range(2):
                S.add("pool", lambda e, hh=hh: e.dma_start(out=wkv[:, :, hh * 512:(hh + 1) * 512],
                                                            in_=wck_v[:, :, hh * 512:(hh + 1) * 512]),
                      writes=["wkv"], dma_sem=s_kv)
            for kc in range(8):
                bk = psrot.next()
                S.add("pe", lambda e, bk=bk, kc=kc: [
                    e.matmul(ps[bk][:, 0:MEM], wkv[:, c, kc * 128:(kc + 1) * 128], mhT[:, c, :],
                             start=(c == 0), stop=(c == KC - 1)) for c in range(KC)][-1],
                    reads=["wkv"] + mk, writes=[("ps", bk)])
                S.add("act", lambda e, bk=bk, kc=kc: e.copy(out=KmT[:, kc, :], in_=ps[bk][:, 0:MEM]),
                      reads=[("ps", bk)], writes=["KmT"])
            for hh in range(2):
                S.add("pool", lambda e, hh=hh: e.dma_start(out=wkv[:, :, hh * 512:(hh + 1) * 512],
                                                            in_=wcv_v[:, :, hh * 512:(hh + 1) * 512]),
                      writes=["wkv"], dma_sem=s_kv)
            for mb in range(2):
                for hh in range(2):
                    bk = psrot.next()
                    S.add("pe", lambda e, bk=bk, mb=mb, hh=hh: [
                        e.matmul(ps[bk][:], mhT[:, c, mb * 128:(mb + 1) * 128], wkv[:, c, hh * 512:(hh + 1) * 512],
                                 start=(c == 0), stop=(c == KC - 1)) for c in range(KC)][-1],
                        reads=["wkv"] + mk, writes=[("ps", bk)])
                    S.add("act", lambda e, bk=bk, mb=mb, hh=hh: e.copy(out=Vm[:, mb, hh * 512:(hh + 1) * 512],
                                                                      in_=ps[bk][:]),
                          reads=[("ps", bk)], writes=["Vm"])
            S.barrier()
            al.off = mark3
            ocT = al([128, 8, 512], BF16)
            XB = [dict(qT=al([128, 2, 512], BF16), PTc=al([128, 2, 512], BF16), rec=al([128, 512], F32))
                  for _ in range(2)]

            class Seg3:
                def __init__(self):
                    self.ops = []

                def add(self, eng, fn, reads=(), writes=()):
                    self.ops.append((eng, fn, list(reads), list(writes)))

            def flush3(segs):
                n = max([len(sg.ops) for sg in segs] + [0])
                for i in range(n):
                    for sg in segs:
                        if i < len(sg.ops):
                            eng, fn, r, w = sg.ops[i]
                            S.add(eng, fn, reads=r, writes=w)

            def head_seg(ch, x, hk, hcT):
                X = XB[x]
                pb = [4 * x + i for i in range(4)]
                sg = Seg3()
                for ee in range(2):
                    col = (ch * 2 + ee) * 128
                    sg.add("pe", lambda e, ee=ee, col=col: [
                        e.matmul(ps[pb[ee]][:], wq[:, c, col:col + 128], hcT[:, c, :],
                                 start=(c == 0), stop=(c == KC - 1)) for c in range(KC)][-1],
                        reads=["wq"] + hk, writes=[("ps", pb[ee])])
                    sg.add("act", lambda e, ee=ee: e.copy(out=X["qT"][:, ee, :], in_=ps[pb[ee]][:]),
                           reads=[("ps", pb[ee])], writes=[("qT", x, ee)])
                for mb in range(2):
                    sg.add("pe", lambda e, mb=mb: [
                        e.matmul(ps[pb[2 + mb]][:], KmT[:, ch * 2 + ee, mb * 128:(mb + 1) * 128], X["qT"][:, ee, :],
                                 start=(ee == 0), stop=(ee == 1)) for ee in range(2)][-1],
                        reads=["KmT", ("qT", x, 0), ("qT", x, 1)], writes=[("ps", pb[2 + mb])])
                    sg.add("act", lambda e, mb=mb: e.activation(out=X["PTc"][:, mb, :], in_=ps[pb[2 + mb]][:],
                                                                func=AF.Exp, scale=1.0 / 16),
                           reads=[("ps", pb[2 + mb])], writes=[("PTc", x, mb)])
                sg.add("pe", lambda e: [
                    e.matmul(ps[pb[0]][:], onesb[:], X["PTc"][:, mb, :], start=(mb == 0), stop=(mb == 1))
                    for mb in range(2)][-1],
                    reads=["onesb", ("PTc", x, 0), ("PTc", x, 1)], writes=[("ps", pb[0])])
                sg.add("dve", lambda e: e.reciprocal(out=X["rec"], in_=ps[pb[0]][:]),
                       reads=[("ps", pb[0])], writes=[("rec", x)])
                for ee in range(2):
                    col = ch * 256 + ee * 128
                    sg.add("pe", lambda e, ee=ee, col=col: [
                        e.matmul(ps[pb[1 + ee]][:], Vm[:, mb, col:col + 128], X["PTc"][:, mb, :],
                                 start=(mb == 0), stop=(mb == 1)) for mb in range(2)][-1],
                        reads=["Vm", ("PTc", x, 0), ("PTc", x, 1)], writes=[("ps", pb[1 + ee])])
                    sg.add("dve", lambda e, ee=ee: e.tensor_tensor(
                        out=ocT[:, ch * 2 + ee, :], in0=ps[pb[1 + ee]][:], in1=X["rec"], op=ALU.mult),
                        reads=[("ps", pb[1 + ee]), ("rec", x)], writes=[("ocT", ch * 2 + ee)])
                return sg

            for t in range(NTO):
                o0 = t * 512
                hcT = hTm[:, :, HC0 + o0:HC0 + o0 + 512]
                hk = [("hcT", t, c) for c in range(KC)]
                for chp in range(2):
                    flush3([head_seg(2 * chp, 0, hk, hcT), head_seg(2 * chp + 1, 1, hk, hcT)])
                for o in range(KC):
                    bk = psrot.next()
                    S.add("pe", lambda e, bk=bk, o=o: [
                        e.matmul(ps[bk][:], woc[:, j, o * 128:(o + 1) * 128], ocT[:, j, :],
                                 start=(j == 0), stop=(j == 7)) for j in range(8)][-1],
                        reads=["woc"] + [("ocT", j) for j in range(8)], writes=[("ps", bk)])
                    resid_add(bk, o, o0, 1.0)
            S.barrier()

        if upto >= 9:
            ffn_phase([(None, t * T) for t in range(NOWN // T)], w2i_d, w2o_d, C_G2)
            S.barrier()

        al = Bump()
        ot = [al([128, 512], F32) for _ in range(2)]
        otrot = Rot(range(2))
        for s in range(NTO):
            o0 = s * 512
            for c in range(KC):
                S.add("act", lambda e, c=c, o0=o0: e.activation(out=sq[:, c, :], in_=xT[:, c, o0:o0 + 512],
                                                                  func=AF.Square),
                      reads=[("x", s, c)], writes=[("sq", c)])
            bk = psrot.next()
            S.add("pe", lambda e, bk=bk: [e.matmul(ps[bk][:], onesm[:], sq[:, c, :], start=(c == 0),
                                                   stop=(c == KC - 1)) for c in range(KC)][-1],
                  reads=["onesm"] + [("sq", c) for c in range(KC)], writes=[("ps", bk)])
            rstd_from_ps(bk, 512, rstd[:])
            for c in range(KC):
                k = otrot.next()
                S.add("dve", lambda e, c=c, k=k, o0=o0: e.scalar_tensor_tensor(
                    out=ot[k], in0=xT[:, c, o0:o0 + 512], scalar=cst[:, C_GF + c:C_GF + c + 1], in1=rstd[:],
                    op0=ALU.mult, op1=ALU.mult),
                    reads=[("x", s, c), "rstd", "cst"], writes=[("ot", k)])
                S.add("sp", lambda e, c=c, k=k, o0=o0: e.dma_start(out=out_v[:, c, o0:o0 + 512], in_=ot[k]),
                      reads=[("ot", k)], writes=[("out", s, c)], dma_sem=s_out[k])
        S.add("sp", lambda e: e.nop(), reads=[("out", s, c) for s in range(NTO) for c in range(KC)])

        with nc.Block() as block:
            S.emit(nc, block, engsem)
    return nc


def _pcols(v):
    return np.ascontiguousarray(np.asarray(v, np.float32).reshape(-1, 128).T)


def _rep(v):
    v = np.asarray(v, np.float32)
    return np.broadcast_to(v[None, :], (128, len(v)))


def _rel_bucket(rel):
    n = np.maximum(-rel, 0)
    nf = np.maximum(n, 1).astype(np.float32)
    large = 16 + (np.log(nf / np.float32(16)) / np.float32(np.log(128 / 16)) * np.float32(16)).astype(np.int32)
    large = np.minimum(large, 31)
    return np.where(n < 16, n, large)


def make_inputs(cfg, inp, n_batch):
    Sq = inp["x"].shape[1]
    assert cfg.NPRE == cfg.NOWN == Sq // 2
    H = Sq // 2
    f32 = np.float32
    cst = np.zeros((128, NCST), f32)
    cst[:, C_G1:C_G1 + 8] = _pcols(inp["norm_ffn1_w"][0])
    cst[:, C_GF:C_GF + 8] = _pcols(inp["norm_final_w"])
    cst[:, C_G2:C_G2 + 8] = _pcols(inp["norm_ffn2_w"][0])
    cst[:, C_GM:C_GM + 8] = _pcols(inp["norm_mix_w"][0])
    cst[:, C_GC:C_GC + 8] = _pcols(inp["norm_cross_w"][0])
    cst[:, C_GMEM:C_GMEM + 8] = _pcols(inp["norm_mem_w"][0])
    cst[:, C_SSDW:C_SSDW + 8] = _pcols(inp["ssd_norm_w"][0])
    cst[:, C_SUBLN] = np.asarray(inp["subln_w"][0], f32)
    cst[:, C_CONVB:C_CONVB + 16] = _pcols(inp["conv_b"][0])
    cw = np.asarray(inp["conv_w"][0], f32)
    cst[:, C_CONVW:C_CONVW + 64] = cw.reshape(4, 16, 128).transpose(2, 1, 0).reshape(128, 64)
    cst[:, C_DTB:C_DTB + 16] = _rep(inp["dt_bias"][0])
    cst[:, C_ALOG:C_ALOG + 16] = _rep(inp["a_log"][0])
    cst[:, C_DSKIP:C_DSKIP + 16] = _rep(inp["d_skip"][0])
    cst[:, C_LQ1:C_LQ1 + 64] = _rep(inp["lambda_q1"][0])
    cst[:, C_LK1:C_LK1 + 64] = _rep(inp["lambda_k1"][0])
    cst[:, C_LQ2:C_LQ2 + 64] = _rep(inp["lambda_q2"][0])
    cst[:, C_LK2:C_LK2 + 64] = _rep(inp["lambda_k2"][0])
    rb = np.asarray(inp["rel_bias"], f32)
    cst[:, C_FAROWN:C_FAROWN + 8] = _rep(rb[31])
    rbx = np.concatenate([rb, np.full((1, 8), NEG, f32)], axis=0)
    k = np.arange(128)[:, None]
    q = np.arange(128)[None, :]
    idxD = np.where(k <= q, _rel_bucket(k - q), 32)
    idxP = _rel_bucket(k - q - 128)
    Dt = rbx[idxD]
    Pt = rbx[idxP]
    negt = np.full_like(Pt, NEG)
    shared = {}
    for nm in ("ffn1_w_in", "ffn1_w_out", "ffn2_w_in", "ffn2_w_out", "w_in_mix", "w_out_mix",
               "w_cq", "w_ck", "w_cv", "w_co"):
        shared[nm] = np.ascontiguousarray(inp[nm][0], dtype=f32)
    maps = []
    for b in range(n_batch):
        xb = np.asarray(inp["x"][b], f32)
        memT = np.ascontiguousarray(np.asarray(inp["mem"][b], f32).T)
        for half in range(2):
            own = xb[half * H:(half + 1) * H]
            pre = xb[0:H]
            m = dict(shared)
            m["xT"] = np.ascontiguousarray(np.concatenate([pre, own], axis=0).T)
            m["memT"] = memT
            c2 = cst.copy()
            c2[:, C_FLAG] = float(half)
            c2[:, C_FARPRE:C_FARPRE + 8] = _rep(rb[31]) if half == 1 else NEG
            m["cst"] = c2
            Pb = Pt if half == 1 else negt
            bt = np.stack([Dt, Pt, Pb], axis=0)
            m["biasT"] = np.ascontiguousarray(bt.transpose(1, 3, 0, 2).reshape(128, 8 * 3 * 128), dtype=f32)
            maps.append(m)
    return maps


def build_dbg(cfg):
    try:
        return build(cfg)
    except _Stop:
        return cfg._nc


def run(cfg, inp, n_batch, trace=False):
    nc = build_dbg(cfg)
    maps = make_inputs(cfg, inp, n_batch)
    n = len(maps)
    res = run_bass_kernel_spmd(nc, maps, core_ids=list(range(n)), trace=trace)
    H = cfg.NOWN
    out = np.zeros((n_batch, 2 * H, cfg.D), np.float32)
    for i, r in enumerate(res.results):
        b, half = i // 2, i % 2
        out[b, half * H:(half + 1) * H] = np.asarray(r["outT"]).T
    return out, res


def kernel(**inputs):
    cfg = Cfg()
    out, _ = run(cfg, inputs, 4)
    return out
```

```python
import numpy as np
from contextlib import ExitStack
import concourse.bass as bass
import concourse.mybir as mybir
from concourse.bass_utils import run_bass_kernel_spmd

F32 = mybir.dt.float32
BF16 = mybir.dt.bfloat16
AF = mybir.ActivationFunctionType
ALU = mybir.AluOpType
AX = mybir.AxisListType

EPS = 1e-6
NEG = -30000.0
LAMBDA_INIT = 0.2

C_G1, C_GF, C_G2, C_GM, C_GC, C_GMEM, C_SSDW, C_SUBLN, C_FLAG = 0, 8, 16, 24, 32, 40, 48, 56, 57
C_CONVB, C_CONVW, C_DTB, C_ALOG, C_DSKIP = 58, 74, 138, 154, 170
C_LQ1, C_LK1, C_LQ2, C_LK2, C_FAROWN, C_FARPRE, NCST = 186, 250, 314, 378, 442, 450, 464
O_Z, O_XBC, O_DT, O_Q, O_K, O_V = 0, 1024, 3072, 3088, 4112, 5136


class _Op:
    __slots__ = ("eng", "fn", "deps", "sem", "val", "needs", "idx", "dma")


class Sched:
    ENGS = ("pe", "act", "dve", "pool", "sp")

    def __init__(self):
        self.ops = {e: [] for e in self.ENGS}
        self.lastw = {}
        self.readers = {}
        self.dmacount = {}
        self.dmasems = {}
        self.bar = None
        self.synced = set(self.ENGS)

    def barrier(self):
        b = []
        for e in self.ENGS:
            last = None
            for o in reversed(self.ops[e]):
                if not o.dma:
                    last = o
                    break
            if last is not None:
                b.append(last)
        for sid, cnt in self.dmacount.items():
            b.append((self.dmasems[sid], cnt))
        self.bar = b
        self.synced = set()

    def add(self, eng, fn, reads=(), writes=(), dma_sem=None):
        o = _Op()
        o.eng, o.fn, o.needs, o.dma = eng, fn, False, dma_sem is not None
        o.idx = len(self.ops[eng])
        o.sem, o.val = dma_sem, 0
        deps = {}

        def dep(p, war):
            if p is o:
                return
            if isinstance(p, tuple):
                q = deps.get(("d", id(p[0])))
                if q is None or q[1] < p[1]:
                    deps[("d", id(p[0]))] = p
                return
            if p.dma:
                dep((p.sem, self.dmacount[id(p.sem)]), war)
                return
            if p.eng == eng and not o.dma and eng == "pe":
                return
            key = ("c", p.eng)
            q = deps.get(key)
            if q is None or p.idx > q.idx:
                deps[key] = p

        if eng not in self.synced:
            self.synced.add(eng)
            for p in self.bar:
                if isinstance(p, tuple) or p.eng != eng:
                    dep(p, False)
        for k in reads:
            w = self.lastw.get(k)
            if w is not None:
                dep(w, False)
            if isinstance(k, tuple) and k[0] == "ps":
                for r in self.readers.get(k, {}).values():
                    if r.eng != eng:
                        dep(r, True)
        for k in writes:
            w = self.lastw.get(k)
            if w is not None:
                dep(w, False)
            for r in self.readers.get(k, {}).values():
                dep(r, True)
        if o.dma:
            self.dmasems[id(dma_sem)] = dma_sem
            self.dmacount[id(dma_sem)] = self.dmacount.get(id(dma_sem), 0) + 16
            o.val = self.dmacount[id(dma_sem)]
        for k in writes:
            self.lastw[k] = o
            self.readers[k] = {}
        for k in reads:
            d = self.readers.setdefault(k, {})
            key = ("d", id(o.sem)) if o.dma else ("c", eng)
            d[key] = o
        o.deps = list(deps.values())
        for p in o.deps:
            if not isinstance(p, tuple):
                p.needs = True
        self.ops[eng].append(o)
        return o

    def emit(self, nc, block, engsem):
        for e in self.ENGS:
            c = 0
            for o in self.ops[e]:
                if not o.dma and o.needs:
                    c += 1
                    o.val = c
                    o.sem = engsem[e]

        def run(ename, eng):
            waited = {}
            for o in self.ops[ename]:
                for p in o.deps:
                    psem, pval = p if isinstance(p, tuple) else (p.sem, p.val)
                    if waited.get(id(psem), 0) < pval:
                        eng.wait_ge(psem, pval)
                        waited[id(psem)] = pval
                ins = o.fn(eng)
                if o.dma:
                    ins.then_inc(o.sem, 16)
                elif o.needs:
                    ins.then_inc(o.sem, 1)

        block.tensor(lambda eng: run("pe", eng))
        block.scalar(lambda eng: run("act", eng))
        block.vector(lambda eng: run("dve", eng))
        block.gpsimd(lambda eng: run("pool", eng))
        block.sync(lambda eng: run("sp", eng))


DBG = []


class _Stop(Exception):
    pass


class Rot:
    def __init__(self, items):
        self.items = list(items)
        self.i = 0

    def next(self):
        x = self.items[self.i % len(self.items)]
        self.i += 1
        return x


class Cfg:
    def __init__(self, npre=2048, nown=2048, tffn=1024, upto=9):
        self.D = 1024
        self.DFF = 2816
        self.NPRE = npre
        self.NOWN = nown
        self.NTOK = npre + nown
        self.TFFN = tffn
        self.upto = upto


def build(cfg):
    nc = bass.Bass("TRN2", target_bir_lowering=False)
    cfg._nc = nc
    D, DFF = cfg.D, cfg.DFF
    NPRE, NOWN, NTOK, T = cfg.NPRE, cfg.NOWN, cfg.NTOK, cfg.TFFN
    KC = D // 128
    FC = DFF // 128
    NS = T // 512
    NB = NTOK // 128
    NBP = NPRE // 128
    NTT = NTOK // 512
    NTO = NOWN // 512
    MEM = 256
    upto = cfg.upto

    def din(name, shape):
        return nc.dram_tensor(name, list(shape), F32, kind="ExternalInput").ap()

    xT_d = din("xT", [D, NTOK])
    memT_d = din("memT", [D, MEM])
    cst_d = din("cst", [128, NCST])
    bias_d = din("biasT", [128, 8 * 3 * 128])
    w1i_d = din("ffn1_w_in", [D, 2 * DFF])
    w1o_d = din("ffn1_w_out", [DFF, D])
    w2i_d = din("ffn2_w_in", [D, 2 * DFF])
    w2o_d = din("ffn2_w_out", [DFF, D])
    wim_d = din("w_in_mix", [D, 6160])
    wom_d = din("w_out_mix", [2 * D, D])
    wcq_d = din("w_cq", [D, D])
    wck_d = din("w_ck", [D, D])
    wcv_d = din("w_cv", [D, D])
    wco_d = din("w_co", [D, D])
    out_d = nc.dram_tensor("outT", [D, NOWN], F32, kind="ExternalOutput").ap()

    S = Sched()
    es = ExitStack()

    def sb(name, shape, dt):
        return es.enter_context(nc.sbuf_tensor("sb_" + name, list(shape), dt))

    def sem(name):
        return es.enter_context(nc.semaphore(name))

    with es:
        HW = max(NTOK + max(0, 2 * T - NOWN), 2048 + NOWN)
        xT = sb("xT_own", [128, KC, NOWN], F32)
        hTm_t = sb("hTm", [128, KC, HW], BF16)
        hTm = hTm_t[:]
        cst = sb("cst", [128, NCST], F32)
        onesm = sb("onesm", [128, 128], BF16)
        onesb = sb("onesb", [128, 128], BF16)
        onesf = sb("onesf", [128, 128], F32)
        ident = sb("ident", [128, 128], BF16)
        identf = sb("identf", [128, 128], F32)
        Umat = sb("Umat", [128, 128], F32)
        Lmat = sb("Lmat", [128, 128], F32)
        sq = sb("sq", [128, KC, 512], BF16)
        rstd = sb("rstd", [128, 512], F32)
        rtmp = sb("rtmp", [128, 512], F32)
        ARW = 16128
        arena = sb("arena", [128, ARW], F32)
        psp = [es.enter_context(nc.psum_tensor(f"psp{i}", [128, 1024], F32)) for i in range(4)]

        class _PsView:
            def __init__(self, ap):
                self.ap = ap

            def __getitem__(self, key):
                return self.ap[key]
        ps = [_PsView(psp[i // 2][:, (i % 2) * 512:(i % 2 + 1) * 512]) for i in range(8)]

        class Bump:
            def __init__(self):
                self.off = 0

            def __call__(self, shape, dt):
                n = 1
                for v in shape[1:]:
                    n *= v
                esz = 2 if dt is BF16 else 4
                nbytes = (n * esz + 31) // 32 * 32
                a = arena[:, self.off // 4:(self.off + nbytes) // 4]
                DBG.append((self.off, list(shape), "bf16" if dt is BF16 else "f32"))
                self.off += nbytes
                assert self.off <= ARW * 4, ("arena overflow", self.off)
                if dt is BF16:
                    a = a.bitcast(BF16)
                if nbytes // esz != n:
                    a = a[:, 0:n]
                if len(shape) == 3:
                    a = a.rearrange("p (a b) -> p a b", a=shape[1])
                elif len(shape) == 4:
                    a = a.rearrange("p (a b c) -> p a b c", a=shape[1], b=shape[2])
                return a

        engsem = {e: sem(f"sem_{e}") for e in Sched.ENGS}
        s_x = [sem(f"s_x{i}") for i in range(NTO)]
        s_cst = sem("s_cst")
        s_out = [sem(f"s_out{i}") for i in range(2)]
        psrot = Rot(range(8))

        xT_v = xT_d.rearrange("(c p) t -> p c t", p=128)
        out_v = out_d.rearrange("(c p) t -> p c t", p=128)
        wim_v = wim_d.rearrange("(c p) n -> p c n", p=128)
        wom_v = wom_d.rearrange("(f p) n -> p f n", p=128)

        S.add("sp", lambda e: e.dma_start(out=cst[:], in_=cst_d), writes=["cst"], dma_sem=s_cst)
        S.add("dve", lambda e: e.memset(onesm[:], 1.0 / D), writes=["onesm"])
        S.add("dve", lambda e: e.memset(onesb[:], 1.0), writes=["onesb"])
        S.add("dve", lambda e: e.memset(onesf[:], 1.0), writes=["onesf"])
        S.add("pool", lambda e: e.memset(Umat[:], 1.0), writes=["Umat"])
        S.add("pool", lambda e: e.memset(Lmat[:], 1.0), writes=["Lmat"])
        S.add("pool", lambda e: e.memset(identf[:], 1.0), writes=["identf"])
        S.add("pool", lambda e: e.affine_select(out=Umat[:], in_=Umat[:], pattern=[[1, 128]],
                                                 compare_op=ALU.is_ge, fill=0.0, base=0, channel_multiplier=-1),
              reads=["Umat"], writes=["Umat"])
        S.add("pool", lambda e: e.affine_select(out=Lmat[:], in_=Lmat[:], pattern=[[-1, 128]],
                                                 compare_op=ALU.is_gt, fill=0.0, base=0, channel_multiplier=1),
              reads=["Lmat"], writes=["Lmat"])
        S.add("pool", lambda e: e.affine_select(out=identf[:], in_=identf[:], pattern=[[1, 128]],
                                                 compare_op=ALU.is_gt, fill=0.0, base=0, channel_multiplier=-1),
              reads=["identf"], writes=["identf"])
        S.add("dve", lambda e: e.tensor_tensor(out=ident[:], in0=Umat[:], in1=identf[:], op=ALU.subtract),
              reads=["Umat", "identf"], writes=["ident"])

        def chk(tag):
            if getattr(cfg, "dbg", "") == tag:
                with nc.Block() as block:
                    S.emit(nc, block, engsem)
                raise _Stop()

        def load_x(tok0, slot0, n):
            for j in range(n // 512):
                a, b = tok0 + j * 512, slot0 + j * 512
                S.add("sp",
                      lambda e, a=a, b=b: e.dma_start(out=xT[:, :, b:b + 512], in_=xT_v[:, :, a:a + 512]),
                      writes=[("x", b // 512, c) for c in range(KC)], dma_sem=s_x[b // 512])

        def rstd_from_ps(bk, n, out_ap):
            S.add("dve", lambda e: e.tensor_scalar_add(out=rtmp[:, :n], in0=ps[bk][:, :n], scalar1=EPS),
                  reads=[("ps", bk)], writes=["rtmp"])
            S.add("act", lambda e: e.activation(out=rtmp[:, :n], in_=rtmp[:, :n], func=AF.Ln),
                  reads=["rtmp"], writes=["rtmp"])
            S.add("act", lambda e: e.activation(out=out_ap, in_=rtmp[:, :n], func=AF.Exp, scale=-0.5),
                  reads=["rtmp"], writes=["rstd"])

        def rms_norm(src_fn, src_keys, gcol, dst_fn, dst_keys, n=512):
            for c in range(KC):
                S.add("act", lambda e, c=c: e.activation(out=sq[:, c, :n], in_=src_fn(c), func=AF.Square),
                      reads=[src_keys[c]], writes=[("sq", c)])
            bk = psrot.next()
            S.add("pe", lambda e, bk=bk: [e.matmul(ps[bk][:, :n], onesm[:], sq[:, c, :n],
                                                   start=(c == 0), stop=(c == KC - 1)) for c in range(KC)][-1],
                  reads=["onesm"] + [("sq", c) for c in range(KC)], writes=[("ps", bk)])
            rstd_from_ps(bk, n, rstd[:, :n])
            for c in range(KC):
                S.add("dve", lambda e, c=c: e.scalar_tensor_tensor(out=dst_fn(c), in0=src_fn(c),
                                                                    scalar=cst[:, gcol + c:gcol + c + 1],
                                                                    in1=rstd[:, :n], op0=ALU.mult, op1=ALU.mult),
                      reads=[src_keys[c], "rstd", "cst"], writes=[dst_keys[c]])

        def hkeys(tok0):
            return [("hm", tok0 // 512, c) for c in range(KC)]

        def resid_add(bk, o, o0, scale):
            S.add("dve", lambda e: e.scalar_tensor_tensor(
                out=xT[:, o, o0:o0 + 512], in0=ps[bk][:], scalar=scale, in1=xT[:, o, o0:o0 + 512],
                op0=ALU.mult, op1=ALU.add),
                reads=[("ps", bk), ("x", o0 // 512, o)], writes=[("x", o0 // 512, o)])

        def ffn_phase(tiles, win_d, wout_d, gcol, post=None):
            al = Bump()
            aT_hi = al([128, FC - KC, T], BF16)
            NWI, NWO = 3, 2
            wi = [al([128, KC, 256], BF16) for _ in range(NWI)]
            wo = [al([128, FC, 128], BF16) for _ in range(NWO)]
            sg = [al([128, 512], F32) for _ in range(2)]
            s_wi = [sem(f"s_wi{i}_{gcol}") for i in range(NWI)]
            s_wo = [sem(f"s_wo{i}_{gcol}") for i in range(NWO)]
            wirot, worot, sgrot = Rot(range(NWI)), Rot(range(NWO)), Rot(range(2))
            H0 = NPRE
            A0 = NPRE + T
            win_v = win_d.rearrange("(c p) n -> p c n", p=128)
            wout_v = wout_d.rearrange("(f p) n -> p f n", p=128)

            def aTf(f, s):
                if f < KC:
                    return hTm[:, f, A0 + s * 512:A0 + (s + 1) * 512]
                return aT_hi[:, f - KC, s * 512:(s + 1) * 512]

            def norm_tile(tok0, slot0):
                if tok0 is not None:
                    load_x(tok0, slot0, T)
                for s in range(NS):
                    o0 = slot0 + s * 512
                    rms_norm(lambda c, o0=o0: xT[:, c, o0:o0 + 512], [("x", o0 // 512, c) for c in range(KC)], gcol,
                             lambda c, s=s: hTm[:, c, H0 + s * 512:H0 + (s + 1) * 512],
                             [("hf", s, c) for c in range(KC)])

            norm_tile(*tiles[0])
            for ti, (tok0, slot0) in enumerate(tiles):
                for f in range(FC):
                    w = wirot.next()
                    for half in range(2):
                        col = half * DFF + f * 128
                        S.add("pool", lambda e, w=w, half=half, col=col: e.dma_start(
                            out=wi[w][:, :, half * 128:(half + 1) * 128], in_=win_v[:, :, col:col + 128]),
                            writes=[("wi", w, half)], dma_sem=s_wi[w])
                    for s in range(NS):
                        bg, bu = psrot.next(), psrot.next()
                        for half, bk in ((0, bg), (1, bu)):
                            S.add("pe", lambda e, w=w, half=half, bk=bk, s=s: [
                                e.matmul(ps[bk][:], wi[w][:, c, half * 128:(half + 1) * 128],
                                         hTm[:, c, H0 + s * 512:H0 + (s + 1) * 512], start=(c == 0),
                                         stop=(c == KC - 1)) for c in range(KC)][-1],
                                reads=[("wi", w, half)] + [("hf", s, c) for c in range(KC)], writes=[("ps", bk)])
                        g = sgrot.next()
                        S.add("act", lambda e, g=g, bg=bg: e.activation(out=sg[g], in_=ps[bg][:], func=AF.Silu),
                              reads=[("ps", bg)], writes=[("sg", g)])
                        S.add("dve", lambda e, g=g, bu=bu, f=f, s=s: e.tensor_tensor(
                            out=aTf(f, s), in0=ps[bu][:], in1=sg[g], op=ALU.mult),
                            reads=[("ps", bu), ("sg", g)], writes=[("aT", f, s)])
                for o in range(KC):
                    w = worot.next()
                    S.add("pool", lambda e, w=w, o=o: e.dma_start(out=wo[w],
                                                                   in_=wout_v[:, :, o * 128:(o + 1) * 128]),
                          writes=[("wo", w)], dma_sem=s_wo[w])
                    for s in range(NS):
                        bk = psrot.next()
                        S.add("pe", lambda e, w=w, bk=bk, s=s: [
                            e.matmul(ps[bk][:], wo[w][:, f, :], aTf(f, s),
                                     start=(f == 0), stop=(f == FC - 1)) for f in range(FC)][-1],
                            reads=[("wo", w)] + [("aT", f, s) for f in range(FC)], writes=[("ps", bk)])
                        resid_add(bk, o, slot0 + s * 512, 0.5)
                    if o == 0 and ti + 1 < len(tiles) and tiles[ti + 1][1] != slot0:
                        norm_tile(*tiles[ti + 1])
                if post is not None:
                    post(tok0, slot0)
                if ti + 1 < len(tiles) and tiles[ti + 1][1] == slot0:
                    norm_tile(*tiles[ti + 1])

        def norm_mix_tile(tok0, slot0, n):
            for s in range(n // 512):
                o0, t0 = slot0 + s * 512, tok0 + s * 512
                rms_norm(lambda c, o0=o0: xT[:, c, o0:o0 + 512], [("x", o0 // 512, c) for c in range(KC)], C_GM,
                         lambda c, t0=t0: hTm[:, c, t0:t0 + 512], hkeys(t0))

        do_mix = upto >= 2
        tiles = []
        if do_mix:
            for pt in range(NPRE // T):
                tiles.append((pt * T, (pt % (NOWN // T)) * T))
        for t in range(NOWN // T):
            tiles.append((NPRE + t * T, t * T))

        def post1(tok0, slot0):
            if tok0 < NPRE:
                norm_mix_tile(tok0, slot0, T)

        ffn_phase(tiles, w1i_d, w1o_d, C_G1, post1)
        S.barrier()

        if do_mix:
            norm_mix_tile(NPRE, 0, NOWN)
            S.barrier()
            chk("c_mixnorm")
            s_wog = sem("s_wog")

            def wout_unit(yT, wog, fc0):
                S.add("pool", lambda e: e.dma_start(out=wog, in_=wom_v[:, fc0:fc0 + 2, :]),
                      writes=["wog"], dma_sem=s_wog)
                for o in range(KC):
                    for t in range(NTO):
                        bk = psrot.next()
                        S.add("pe", lambda e, bk=bk, o=o, t=t: [
                            e.matmul(ps[bk][:], wog[:, j, o * 128:(o + 1) * 128], yT[:, j, t * 512:(t + 1) * 512],
                                     start=(j == 0), stop=(j == 1)) for j in range(2)][-1],
                            reads=["wog", ("yT", 0), ("yT", 1)], writes=[("ps", bk)])
                        resid_add(bk, o, t * 512, 1.0)

            al = Bump()
            wdt = al([128, KC, 16], BF16)
            dtb = al([128, NB, 16], F32)
            la = al([128, NB, 16], F32)
            dte = al([128, NB, 16], F32)
            elc = al([128, NB, 16], F32)
            cd = al([128, NB, 16], F32)
            mark = al.off
            lc = al([128, NB, 16], F32)
            tA = al([128, 16], F32)
            tmpb = al([128, NB, 16], F32)
            tmpc = al([128, NB, 16], F32)
            al.off = mark
            wg = al([128, KC, 768], BF16)
            wog = al([128, 2, 1024], BF16)
            yT = sq[:].rearrange("p c t -> p (c t)").rearrange("p (i t) -> p i t", i=2)
            pre_sb = al([128, 4, 516], BF16)
            dg = al([128, 4, 4, 128], BF16)
            xc = al([128, 4, 512], BF16)
            CB = []
            for _b in range(2):
                CB.append(dict(
                    Btok=al([128, 128], BF16), xdt=al([128, 256], BF16), xdtd=al([128, 256], BF16),
                    ysk=al([128, 256], F32), cbm=al([128, 128], F32), Lm=al([128, 4, 128], F32),
                    MT=al([128, 4, 128], BF16), y1=al([128, 256], F32), y2=al([128, 256], F32),
                    ss=al([128, 4], F32), yn=al([128, 256], BF16)))
            zst = al([128, 4, 256], F32)
            Sf = al([128, 256], F32)
            S1 = al([128, 256], F32)
            Sbf = al([128, 256], BF16)
            s_wdt = sem("s_wdt")
            s_wg = sem("s_wg")
            NBW = NB * 16

            S.add("pool", lambda e: e.dma_start(out=wdt, in_=wim_v[:, :, O_DT:O_DT + 16]),
                  writes=["wdt"], dma_sem=s_wdt)
            bdt = psrot.next()
            for blk in range(NB):
                S.add("pe", lambda e, blk=blk: [
                    e.matmul(ps[bdt][:, blk * 16:(blk + 1) * 16], hTm[:, c, blk * 128:(blk + 1) * 128], wdt[:, c, :],
                             start=(c == 0), stop=(c == KC - 1)) for c in range(KC)][-1],
                    reads=["wdt"] + hkeys(blk * 128), writes=[("ps", bdt)])

            chk("c_dtraw")

            def v3(a):
                return a.rearrange("p (b h) -> p b h", h=16)

            def bc16(col):
                return cst[:, col:col + 16].unsqueeze(1).to_broadcast([128, NB, 16])

            S.add("dve", lambda e: e.tensor_tensor(out=tmpb, in0=v3(ps[bdt][:, :NBW]), in1=bc16(C_DTB), op=ALU.add),
                  reads=[("ps", bdt), "cst"], writes=["tmpb"])
            S.add("dve", lambda e: e.tensor_scalar_mul(out=tmpc, in0=tmpb, scalar1=-1.0),
                  reads=["tmpb"], writes=["tmpc"])
            S.add("dve", lambda e: e.tensor_tensor(out=tmpc, in0=tmpc, in1=tmpb, op=ALU.min),
                  reads=["tmpb", "tmpc"], writes=["tmpc"])
            S.add("act", lambda e: e.activation(out=tmpc, in_=tmpc, func=AF.Exp),
                  reads=["tmpc"], writes=["tmpc"])
            S.add("dve", lambda e: e.tensor_scalar_add(out=tmpc, in0=tmpc, scalar1=1.0),
                  reads=["tmpc"], writes=["tmpc"])
            S.add("act", lambda e: e.activation(out=tmpc, in_=tmpc, func=AF.Ln),
                  reads=["tmpc"], writes=["tmpc"])
            S.add("dve", lambda e: e.scalar_tensor_tensor(out=dtb, in0=tmpb, scalar=0.0, in1=tmpc,
                                                           op0=ALU.max, op1=ALU.add),
                  reads=["tmpb", "tmpc"], writes=["dtb"])
            chk("c_softplus")
            S.add("act", lambda e: e.activation(out=tA, in_=cst[:, C_ALOG:C_ALOG + 16], func=AF.Exp),
                  reads=["cst"], writes=["tA"])
            S.add("dve", lambda e: e.scalar_tensor_tensor(
                out=la, in0=dtb, scalar=-1.0, in1=tA.unsqueeze(1).to_broadcast([128, NB, 16]),
                op0=ALU.mult, op1=ALU.mult), reads=["dtb", "tA"], writes=["la"])
            chk("c_la")
            blc, btot = psrot.next(), psrot.next()
            for blk in range(NB):
                S.add("pe", lambda e, blk=blk: e.matmul(ps[blc][:, blk * 16:(blk + 1) * 16], Umat[:], la[:, blk, :],
                                                         start=True, stop=True),
                      reads=["Umat", "la"], writes=[("ps", blc)])
                S.add("pe", lambda e, blk=blk: e.matmul(ps[btot][:, blk * 16:(blk + 1) * 16], onesf[:], la[:, blk, :],
                                                         start=True, stop=True),
                      reads=["onesf", "la"], writes=[("ps", btot)])
            chk("c_lcmm")
            S.add("dve", lambda e: e.tensor_copy(out=lc, in_=v3(ps[blc][:, :NBW])),
                  reads=[("ps", blc)], writes=["lc"])
            chk("c_lccopy")
            S.add("act", lambda e: e.activation(out=elc, in_=lc, func=AF.Exp),
                  reads=["lc"], writes=["elc"])
            chk("c_elc")
            S.add("act", lambda e: e.activation(out=cd.rearrange("p b h -> p (b h)"), in_=ps[btot][:, :NBW],
                                                func=AF.Exp),
                  reads=[("ps", btot)], writes=["cd"])
            chk("c_cd")
            S.add("dve", lambda e: e.tensor_tensor(out=dte, in0=v3(ps[btot][:, :NBW]), in1=lc, op=ALU.subtract),
                  reads=[("ps", btot), "lc"], writes=["dte"])
            chk("c_dtesub")
            S.add("act", lambda e: e.activation(out=dte, in_=dte, func=AF.Exp), reads=["dte"], writes=["dte"])

            S.barrier()
            chk("c_tables")

            def bc64(a):
                return a.unsqueeze(2).to_broadcast([128, 4, 64])

            def h4(a):
                return a.rearrange("p (h d) -> p h d", h=4)

            class Seg:
                def __init__(self):
                    self.ops = []

                def add(self, eng, fn, reads=(), writes=()):
                    self.ops.append((eng, fn, list(reads), list(writes)))

            def flush(segs):
                n = max([len(sg.ops) for sg in segs] + [0])
                for i in range(n):
                    for sg in segs:
                        if i < len(sg.ops):
                            eng, fn, r, w = sg.ops[i]
                            S.add(eng, fn, reads=r, writes=w)

            def seg_A(g, j, bi, wgk):
                blk = j * 4 + bi
                own = blk >= NBP
                b = blk % 2
                B_ = CB[b]
                cs = slice(bi * 128, (bi + 1) * 128)
                hs = slice(4 * g, 4 * g + 4)
                sg = Seg()
                btp = 2
                tp = ps[btp][:].bitcast(BF16)
                sg.add("pe", lambda e: [e.transpose(tp[:, i * 128:(i + 1) * 128], xc[:, i, cs], ident[:])
                                        for i in range(3)][-1],
                       reads=["ident", ("xc", 0), ("xc", 1), ("xc", 2)], writes=[("ps", btp)])
                sg.add("act", lambda e: e.copy(out=B_["Btok"], in_=tp[:, 256:384]),
                       reads=[("ps", btp)], writes=[("Btok", b)])
                sg.add("dve", lambda e: e.tensor_tensor(out=h4(B_["xdt"]), in0=h4(tp[:, 0:256]),
                                                        in1=bc64(dtb[:, blk, hs]), op=ALU.mult),
                       reads=[("ps", btp), "dtb"], writes=[("xdt", b)])
                sg.add("pool", lambda e: e.tensor_tensor(out=h4(B_["xdtd"]), in0=h4(B_["xdt"]),
                                                        in1=bc64(dte[:, blk, hs]), op=ALU.mult),
                       reads=[("xdt", b), "dte"], writes=[("xdtd", b)])
                if own:
                    sg.add("dve", lambda e: e.tensor_tensor(
                        out=h4(B_["ysk"]), in0=h4(tp[:, 0:256]),
                        in1=bc64(cst[:, C_DSKIP + 4 * g:C_DSKIP + 4 * g + 4]), op=ALU.mult),
                        reads=[("ps", btp), "cst"], writes=[("ysk", b)])
                bst = b
                sg.add("pe", lambda e: e.matmul(ps[bst][:, 0:256], B_["Btok"], B_["xdtd"], start=True, stop=True),
                       reads=[("Btok", b), ("xdtd", b)], writes=[("ps", bst)])
                if own:
                    bcb = 3
                    sg.add("pe", lambda e: e.matmul(ps[bcb][:, 0:128], xc[:, 2, cs], xc[:, 3, cs],
                                                    start=True, stop=True),
                           reads=[("xc", 2), ("xc", 3)], writes=[("ps", bcb)])
                    sg.add("dve", lambda e: e.tensor_tensor(out=B_["cbm"], in0=ps[bcb][:, 0:128], in1=Umat[:],
                                                            op=ALU.mult),
                           reads=[("ps", bcb), "Umat"], writes=[("cbm", b)])
                    sg.add("pool", lambda e: e.tensor_tensor(
                        out=B_["Lm"], in0=Lmat[:].unsqueeze(1).to_broadcast([128, 4, 128]),
                        in1=la[:, blk, hs].unsqueeze(2).to_broadcast([128, 4, 128]), op=ALU.mult),
                        reads=["Lmat", "la"], writes=[("Lm", b, h) for h in range(4)])
                    bsg = 4
                    sg.add("pe", lambda e: [e.matmul(ps[bsg][:, h * 128:(h + 1) * 128], B_["Lm"][:, h, :], Umat[:],
                                                     start=True, stop=True) for h in range(4)][-1],
                           reads=["Umat"] + [("Lm", b, h) for h in range(4)], writes=[("ps", bsg)])
                    sg.add("act", lambda e: e.activation(out=B_["Lm"].rearrange("p h s -> p (h s)"), in_=ps[bsg][:],
                                                         func=AF.Exp),
                           reads=[("ps", bsg)], writes=[("Lm", b, h) for h in range(4)])
                    sg.add("pool", lambda e: e.tensor_tensor(
                        out=B_["MT"], in0=B_["Lm"], in1=B_["cbm"].unsqueeze(1).to_broadcast([128, 4, 128]),
                        op=ALU.mult),
                        reads=[("Lm", b, h) for h in range(4)] + [("cbm", b)], writes=[("MT", b)])
                return sg, bst

            def seg_B(g, j, bi, wgk, bst):
                blk = j * 4 + bi
                own = blk >= NBP
                b = blk % 2
                B_ = CB[b]
                cs = slice(bi * 128, (bi + 1) * 128)
                hs = slice(4 * g, 4 * g + 4)
                sg = Seg()
                if own:
                    by = 5

                    def fny(e):
                        for h in range(4):
                            e.matmul(ps[by][:, h * 64:(h + 1) * 64], B_["MT"][:, h, :],
                                     B_["xdt"][:, h * 64:(h + 1) * 64], start=True, stop=True)
                        return e.matmul(ps[by][:, 256:512], xc[:, 3, cs], Sbf, start=True, stop=True)
                    sg.add("pe", fny, reads=[("MT", b), ("xdt", b), ("xc", 3), "Sbf"], writes=[("ps", by)])
                if blk < NB - 1:
                    sg.add("pool", lambda e: e.tensor_tensor(out=h4(S1), in0=h4(Sf), in1=bc64(cd[:, blk, hs]),
                                                            op=ALU.mult),
                           reads=["Sf", "cd"], writes=["S1"])
                    sg.add("dve", lambda e: e.tensor_tensor(out=Sf, in0=ps[bst][:, 0:256], in1=S1, op=ALU.add),
                           reads=[("ps", bst), "S1"], writes=["Sf"])
                    if blk == NBP - 1:
                        sg.add("dve", lambda e: e.tensor_scalar_mul(out=Sf, in0=Sf, scalar1=cst[:, C_FLAG:C_FLAG + 1]),
                               reads=["Sf", "cst"], writes=["Sf"])
                    sg.add("act", lambda e: e.copy(out=Sbf, in_=Sf), reads=["Sf"], writes=["Sbf"])
                if own:
                    sg.add("dve", lambda e: e.tensor_tensor(out=h4(B_["y1"]), in0=h4(ps[by][:, 256:512]),
                                                            in1=bc64(elc[:, blk, hs]), op=ALU.mult),
                           reads=[("ps", by), "elc"], writes=[("y1", b)])
                    sg.add("dve", lambda e: e.tensor_tensor(out=B_["y2"], in0=ps[by][:, 0:256], in1=B_["y1"],
                                                            op=ALU.add),
                           reads=[("ps", by), ("y1", b)], writes=[("y2", b)])
                    sg.add("dve", lambda e: e.tensor_tensor(out=B_["y2"], in0=B_["y2"], in1=B_["ysk"], op=ALU.add),
                           reads=[("y2", b), ("ysk", b)], writes=[("y2", b)])
                    sg.add("dve", lambda e: e.tensor_tensor(out=B_["y2"], in0=B_["y2"], in1=zst[:, bi, :], op=ALU.mult),
                           reads=[("y2", b), ("zst", bi)], writes=[("y2", b)])
                    sg.add("pool", lambda e: e.tensor_tensor(out=B_["y1"], in0=B_["y2"], in1=B_["y2"], op=ALU.mult),
                           reads=[("y2", b)], writes=[("y1", b)])
                    ss = B_["ss"]
                    sg.add("dve", lambda e: e.reduce_sum(out=ss[:, 0:1], in_=B_["y1"], axis=AX.X),
                           reads=[("y1", b)], writes=[("ss", b)])
                    sg.add("dve", lambda e: e.tensor_scalar(out=ss[:, 1:2], in0=ss[:, 0:1], scalar1=1.0 / 256,
                                                            scalar2=EPS, op0=ALU.mult, op1=ALU.add),
                           reads=[("ss", b)], writes=[("ss", b)])
                    sg.add("act", lambda e: e.activation(out=ss[:, 2:3], in_=ss[:, 1:2], func=AF.Ln),
                           reads=[("ss", b)], writes=[("ss", b)])
                    sg.add("act", lambda e: e.activation(out=ss[:, 3:4], in_=ss[:, 2:3], func=AF.Exp, scale=-0.5),
                           reads=[("ss", b)], writes=[("ss", b)])
                    sg.add("dve", lambda e: e.tensor_scalar_mul(out=B_["yn"], in0=B_["y2"], scalar1=ss[:, 3:4]),
                           reads=[("y2", b), ("ss", b)], writes=[("yn", b)])
                return sg

            def seg_C(g, j, bi):
                blk = j * 4 + bi
                own = blk >= NBP
                b = blk % 2
                B_ = CB[b]
                sg = Seg()
                if own:
                    ob = blk - NBP
                    bt2 = 7
                    tp2 = ps[bt2][:].bitcast(BF16)
                    sg.add("pe", lambda e: [e.transpose(tp2[:, i * 128:(i + 1) * 128],
                                                        B_["yn"][:, i * 128:(i + 1) * 128], ident[:])
                                            for i in range(2)][-1],
                           reads=["ident", ("yn", b)], writes=[("ps", bt2)])
                    for i in range(2):
                        sg.add("dve", lambda e, i=i: e.tensor_scalar_mul(
                            out=yT[:, i, ob * 128:(ob + 1) * 128], in0=tp2[:, i * 128:(i + 1) * 128],
                            scalar1=cst[:, C_SSDW + 2 * g + i:C_SSDW + 2 * g + i + 1]),
                            reads=[("ps", bt2), "cst"], writes=[("yT", i)])
                return sg

            for g in range(4):
                for (dst0, src0, n) in ((0, O_XBC + g * 256, 256), (256, O_XBC + 1024 + g * 128, 128),
                                        (384, O_XBC + 1536 + g * 128, 128), (512, O_Z + g * 256, 256)):
                    S.add("pool", lambda e, dst0=dst0, src0=src0, n=n: e.dma_start(
                        out=wg[:, :, dst0:dst0 + n], in_=wim_v[:, :, src0:src0 + n]),
                        writes=[("wg", dst0)], dma_sem=s_wg)
                wgk = [("wg", 0), ("wg", 256), ("wg", 384), ("wg", 512)]
                chunk16 = [2 * g, 2 * g + 1, 8 + g, 12 + g]
                S.add("dve", lambda e: e.memset(Sf, 0.0), writes=["Sf"])
                S.add("dve", lambda e: e.memset(Sbf, 0.0), writes=["Sbf"])
                S.add("dve", lambda e: e.memset(pre_sb[:, :, 0:3], 0.0), writes=[("pre", cc) for cc in range(4)])
                for cc in range(4):
                    for tap in range(4):
                        wcol = C_CONVW + chunk16[cc] * 4 + tap
                        S.add("dve", lambda e, cc=cc, tap=tap, wcol=wcol: e.tensor_scalar_mul(
                            out=dg[:, cc, tap, :], in0=ident[:], scalar1=cst[:, wcol:wcol + 1]),
                            reads=["ident", "cst"], writes=[("dg", cc)])
                def conv_stage(j, g=g, wgk=wgk, chunk16=chunk16):
                    t0 = j * 512
                    lastpre = NPRE // 512 - 1
                    need_pre = [True, True, True, j >= lastpre]
                    need_cv = [True, True, True, j > lastpre]
                    bks = [psrot.next() for _ in range(4)]
                    bcs = [psrot.next() for _ in range(4)]
                    for cc in range(4):
                        if not need_pre[cc]:
                            continue
                        bk = bks[cc]
                        S.add("pe", lambda e, bk=bk, cc=cc, t0=t0: [
                            e.matmul(ps[bk][:], wg[:, c, cc * 128:(cc + 1) * 128], hTm[:, c, t0:t0 + 512],
                                     start=(c == 0), stop=(c == KC - 1)) for c in range(KC)][-1],
                            reads=wgk + hkeys(t0), writes=[("ps", bk)])
                    for cc in range(4):
                        if not need_pre[cc]:
                            continue
                        bk = bks[cc]
                        if j > 0 and (cc < 3 or j > lastpre):
                            S.add("act", lambda e, cc=cc: e.copy(out=pre_sb[:, cc, 0:3], in_=pre_sb[:, cc, 512:515]),
                                  reads=[("pre", cc)], writes=[("pre", cc)])
                            if j == NPRE // 512:
                                S.add("dve", lambda e, cc=cc: e.tensor_scalar_mul(
                                    out=pre_sb[:, cc, 0:3], in0=pre_sb[:, cc, 0:3], scalar1=cst[:, C_FLAG:C_FLAG + 1]),
                                    reads=[("pre", cc), "cst"], writes=[("pre", cc)])
                        S.add("act", lambda e, bk=bk, cc=cc: e.copy(out=pre_sb[:, cc, 3:515], in_=ps[bk][:]),
                              reads=[("ps", bk), ("pre", cc)], writes=[("pre", cc)])
                    for cc in range(4):
                        if not need_cv[cc]:
                            continue
                        bc_ = bcs[cc]
                        S.add("pe", lambda e, bc_=bc_, cc=cc: [
                            e.matmul(ps[bc_][:], dg[:, cc, tap, :], pre_sb[:, cc, tap:tap + 512],
                                     start=(tap == 0), stop=(tap == 3)) for tap in range(4)][-1],
                            reads=[("dg", cc), ("pre", cc)], writes=[("ps", bc_)])
                    for cc in range(4):
                        if not need_cv[cc]:
                            continue
                        bc_ = bcs[cc]
                        bcol = C_CONVB + chunk16[cc]
                        S.add("act", lambda e, cc=cc, bcol=bcol, bc_=bc_: e.activation(
                            out=xc[:, cc, :], in_=ps[bc_][:], func=AF.Silu, bias=cst[:, bcol:bcol + 1]),
                            reads=[("ps", bc_), "cst"], writes=[("xc", cc)])
                    if j * 4 >= NBP:
                        for hb in range(2):
                            bzz = psrot.next()
                            for q2 in range(2):
                                bi_ = hb * 2 + q2
                                blk_ = j * 4 + bi_
                                S.add("pe", lambda e, bzz=bzz, q2=q2, blk_=blk_: [
                                    e.matmul(ps[bzz][:, q2 * 256:(q2 + 1) * 256],
                                             hTm[:, c, blk_ * 128:(blk_ + 1) * 128], wg[:, c, 512:768],
                                             start=(c == 0), stop=(c == KC - 1)) for c in range(KC)][-1],
                                    reads=wgk + hkeys(blk_ * 128), writes=[("ps", bzz)])
                            S.add("act", lambda e, bzz=bzz, hb=hb: e.activation(
                                out=zst[:, hb * 2:hb * 2 + 2, :].rearrange("p a b -> p (a b)"), in_=ps[bzz][:],
                                func=AF.Silu),
                                reads=[("ps", bzz)], writes=[("zst", hb * 2), ("zst", hb * 2 + 1)])

                def split(sg, n):
                    a, b2 = Seg(), Seg()
                    a.ops, b2.ops = sg.ops[:n], sg.ops[n:]
                    return a, b2

                conv_stage(0)
                for j in range(NTT):
                    if g == 0 and j == 0:
                        chk("c_conv")
                    SA, SB, SC = [], [], []
                    for bi in range(4):
                        sgA, bst = seg_A(g, j, bi, wgk)
                        own = (j * 4 + bi) >= NBP
                        SA.append(split(sgA, 6 if own else 5))
                        sgB = seg_B(g, j, bi, wgk, bst)
                        nb1 = len(sgB.ops) - 6 if own else len(sgB.ops)
                        SB.append(split(sgB, nb1))
                        SC.append(seg_C(g, j, bi))

                    def do_round(r):
                        segs = []
                        if r < 4:
                            segs += list(SA[r])
                        if 0 <= r - 1 < 4:
                            segs.append(SB[r - 1][0])
                        if 0 <= r - 2 < 4:
                            segs.append(SB[r - 2][1])
                        if 0 <= r - 3 < 4:
                            segs.append(SC[r - 3])
                        flush(segs)
                    for r in range(5):
                        do_round(r)
                    if j + 1 < NTT:
                        conv_stage(j + 1)
                    for r in range(5, 7):
                        do_round(r)
                wout_unit(yT, wog, 2 * g)
            S.barrier()

            if getattr(cfg, "dbg", "") == "ssd":
                with nc.Block() as block:
                    S.emit(nc, block, engsem)
                return nc
            al = Bump()
            whs = [al([128, KC, 384], BF16) for _ in range(2)]
            KT = al([128, NTOK], BF16)
            QT = al([128, NOWN], BF16)
            V = al([128, NB, 130], BF16)
            btf = al([128, 3, 128], F32)
            btm = al([128, 3, 128], BF16)
            PTm = [al([128, 2, 512], BF16) for _ in range(2)]
            PT = [[PTm[kk][:, c, :] for kk in range(2)] for c in range(2)]
            wog2 = al([128, 2, 1024], BF16)
            yT2 = al([128, 2, NOWN], BF16)
            lt = al([128, 64], F32)
            lv = al([128, 8], F32)
            rc = al([128, 4, 4], F32)
            rs = al([128, 16], F32)
            t2 = al([128, 4, 128], F32)
            ov = al([128, 4, 128], F32)
            osq = al([128, 4, 128], F32)
            on = al([128, 4, 128], BF16)
            pending2 = []
            s_whs = [sem("s_wh0"), sem("s_wh1")]
            s_bt = sem("s_bt")
            if getattr(cfg, "dbg", "") == "preamble0":
                with nc.Block() as block:
                    S.emit(nc, block, engsem)
                return nc
            for i, (ca, cb) in enumerate(((C_LQ1, C_LK1), (C_LQ2, C_LK2))):
                S.add("dve", lambda e, ca=ca, cb=cb: e.tensor_tensor(out=lt, in0=cst[:, ca:ca + 64],
                                                                      in1=cst[:, cb:cb + 64], op=ALU.mult),
                      reads=["cst"], writes=["lt"])
                S.add("dve", lambda e, i=i: e.reduce_sum(out=lv[:, i:i + 1], in_=lt, axis=AX.X),
                      reads=["lt"], writes=[("lv", i)])
            S.add("act", lambda e: e.activation(out=lv[:, 2:4], in_=lv[:, 0:2], func=AF.Exp),
                  reads=[("lv", 0), ("lv", 1)], writes=[("lv", 2)])
            S.add("dve", lambda e: e.tensor_tensor(out=lv[:, 4:5], in0=lv[:, 3:4], in1=lv[:, 2:3], op=ALU.subtract),
                  reads=[("lv", 2)], writes=[("lv", 4)])
            S.add("dve", lambda e: e.tensor_scalar_add(out=lv[:, 5:6], in0=lv[:, 4:5], scalar1=-LAMBDA_INIT),
                  reads=[("lv", 4)], writes=[("lv", 5)])
            if getattr(cfg, "dbg", "") == "preamble1":
                with nc.Block() as block:
                    S.emit(nc, block, engsem)
                return nc
            S.add("dve", lambda e: e.memset(V[:, :, 128:130], 1.0), writes=["Vones"])
            bias_v = bias_d.rearrange("p (h t q) -> p h t q", h=8, t=3)
            srot = Rot([0, 1])
            prot = Rot([0, 1, 2, 3, 7])
            OB = [4, 5, 6]

            def po(c, r):
                k = c * 4 + r
                return ps[OB[k // 3]][:, (k % 3) * 130:(k % 3) * 130 + 130]

            def pokey(c, r):
                return ("ps", OB[(c * 4 + r) // 3])

            NH_ = int(getattr(cfg, "nheads", 8))

            def load_wh(hh):
                for (dst0, src0) in ((0, O_Q + hh * 128), (128, O_K + hh * 128), (256, O_V + hh * 128)):
                    S.add("pool", lambda e, dst0=dst0, src0=src0, hh=hh: e.dma_start(
                        out=whs[hh % 2][:, :, dst0:dst0 + 128], in_=wim_v[:, :, src0:src0 + 128]),
                        writes=[("wh", hh % 2, dst0)], dma_sem=s_whs[hh % 2])

            if NH_ > 0:
                load_wh(0)
            for h in range(NH_):
                wh = whs[h % 2]
                whk = h % 2
                if h + 1 < NH_:
                    load_wh(h + 1)
                S.add("sp", lambda e, h=h: e.dma_start(out=btf, in_=bias_v[:, h, :, :]),
                      writes=["btf"], dma_sem=s_bt)
                for t in range(3):
                    col = (C_FAROWN if t < 2 else C_FARPRE) + h
                    S.add("dve", lambda e, t=t, col=col: e.tensor_scalar(
                        out=btm[:, t, :], in0=btf[:, t, :], scalar1=cst[:, col:col + 1], scalar2=8.0,
                        op0=ALU.subtract, op1=ALU.mult),
                        reads=["btf", "cst"], writes=[("btm", t)])
                evrot = Rot(["act", "dve"])

                def evac(dst, src):
                    eng = evrot.next()
                    if eng == "act":
                        S.add("act", lambda e: e.copy(out=dst, in_=src[1]), reads=[("ps", src[0])], writes=["kqv"])
                    else:
                        S.add("dve", lambda e: e.tensor_copy(out=dst, in_=src[1]), reads=[("ps", src[0])],
                              writes=["kqv"])

                for j in range(NTT):
                    bk = prot.next()
                    S.add("pe", lambda e, bk=bk, j=j, wh=wh: [
                        e.matmul(ps[bk][:], wh[:, c, 128:256], hTm[:, c, j * 512:(j + 1) * 512],
                                 start=(c == 0), stop=(c == KC - 1)) for c in range(KC)][-1],
                        reads=[("wh", whk, 128)] + hkeys(j * 512), writes=[("ps", bk)])
                    evac(KT[:, j * 512:(j + 1) * 512], (bk, ps[bk][:]))
                for j in range(NTO):
                    bk = prot.next()
                    S.add("pe", lambda e, bk=bk, j=j, wh=wh: [
                        e.matmul(ps[bk][:], wh[:, c, 0:128], hTm[:, c, NPRE + j * 512:NPRE + (j + 1) * 512],
                                 start=(c == 0), stop=(c == KC - 1)) for c in range(KC)][-1],
                        reads=[("wh", whk, 0)] + hkeys(NPRE + j * 512), writes=[("ps", bk)])
                    evac(QT[:, j * 512:(j + 1) * 512], (bk, ps[bk][:]))
                for j in range(NTT):
                    bk = prot.next()
                    for bi in range(4):
                        blk = j * 4 + bi
                        S.add("pe", lambda e, bk=bk, bi=bi, blk=blk, wh=wh: [
                            e.matmul(ps[bk][:, bi * 128:(bi + 1) * 128], hTm[:, c, blk * 128:(blk + 1) * 128],
                                     wh[:, c, 256:384], start=(c == 0), stop=(c == KC - 1))
                            for c in range(KC)][-1],
                            reads=[("wh", whk, 256)] + hkeys(blk * 128), writes=[("ps", bk)])
                    S.add("dve", lambda e, bk=bk, j=j: e.tensor_copy(
                        out=V[:, j * 4:(j + 1) * 4, 0:128], in_=ps[bk][:].rearrange("p (b e) -> p b e", b=4)),
                        reads=[("ps", bk)], writes=["kqv"])

                for qc in range(NTO):
                    kbs = [(kb, True) for kb in range(NBP)] + [(NBP + jj, False) for jj in range(4 * qc + 4)]

                    def emit_S(ki, kb, is_pre, qc=qc, h=h):
                        jj = kb - NBP
                        r0 = 0 if (is_pre or jj < 4 * qc) else jj - 4 * qc
                        qa = r0 * 128
                        pair = srot.next()
                        for c in range(2):
                            sbk = 2 * pair + c
                            spec = []
                            if is_pre:
                                if kb == NBP - 1 and qc == 0:
                                    spec.append((0, 2))
                            else:
                                r = jj - 4 * qc
                                if 0 <= r < 4:
                                    spec.append((r, 0))
                                if 0 <= r + 1 < 4:
                                    spec.append((r + 1, 1))

                            def fn(e, c=c, sbk=sbk, kb=kb, qa=qa, qc=qc, spec=spec):
                                ins = e.matmul(ps[sbk][:, qa:512], KT[c * 64:(c + 1) * 64, kb * 128:(kb + 1) * 128],
                                               QT[c * 64:(c + 1) * 64, qc * 512 + qa:qc * 512 + 512],
                                               start=True, stop=(len(spec) == 0))
                                for i, (r, t) in enumerate(spec):
                                    ins = e.matmul(ps[sbk][:, r * 128:(r + 1) * 128], ident[:], btm[:, t, :],
                                                   start=False, stop=(i == len(spec) - 1))
                                return ins
                            S.add("pe", fn, reads=["kqv", "ident"] + [("btm", t) for (_, t) in spec],
                                  writes=[("ps", sbk)])
                            pt = PT[c][ki % 2]
                            bcol = (C_FARPRE if is_pre else C_FAROWN) + h
                            if qa > 0:
                                S.add("act", lambda e, pt=pt, sbk=sbk, qa=qa, bcol=bcol: e.activation(
                                    out=pt[:, qa:512], in_=ps[sbk][:, qa:512], func=AF.Exp,
                                    bias=cst[:, bcol:bcol + 1], scale=0.125),
                                    reads=[("ps", sbk), "cst"], writes=[("PT", c, ki % 2)])
                        if qa == 0:
                            S.add("act", lambda e, pair=pair, kk=ki % 2, bcol=bcol: e.activation(
                                out=PTm[kk].rearrange("p c q -> p (c q)"), in_=psp[pair][:, :], func=AF.Exp,
                                bias=cst[:, bcol:bcol + 1], scale=0.125),
                                reads=[("ps", 2 * pair), ("ps", 2 * pair + 1), "cst"],
                                writes=[("PT", 0, ki % 2), ("PT", 1, ki % 2)])

                    def emit_PV(ki, kb, is_pre, qc=qc):
                        jj = kb - NBP
                        r0 = 0 if (is_pre or jj < 4 * qc) else jj - 4 * qc
                        for c in range(2):
                            pt = PT[c][ki % 2]

                            def fn2(e, c=c, pt=pt, kb=kb, r0=r0, ki=ki, jj=jj, qc=qc, is_pre=is_pre):
                                ins = None
                                for r in range(r0, 4):
                                    last = (not is_pre) and (jj == 4 * qc + r)
                                    ins = e.matmul(po(c, r), pt[:, r * 128:(r + 1) * 128], V[:, kb, :],
                                                   start=(ki == 0 and (c * 4 + r) % 3 == 0), stop=last,
                                                   skip_group_check=True)
                                return ins
                            S.add("pe", fn2, reads=[("PT", c, ki % 2), "kqv", "Vones"],
                                  writes=sorted({pokey(c, r) for r in range(r0, 4)}))

                    for ki, (kb, is_pre) in enumerate(kbs):
                        emit_S(ki, kb, is_pre)
                        if ki > 0:
                            emit_PV(ki - 1, *kbs[ki - 1])
                        if ki == 3:
                            while pending2:
                                pending2.pop(0)()
                    emit_PV(len(kbs) - 1, *kbs[-1])
                    for r in range(4):
                        S.add("dve", lambda e, r=r: e.reciprocal(out=rc[:, r, 0:1], in_=po(0, r)[:, 128:129]),
                              reads=[pokey(0, r)], writes=[("rc", r)])
                        S.add("dve", lambda e, r=r: e.reciprocal(out=rc[:, r, 1:2], in_=po(1, r)[:, 128:129]),
                              reads=[pokey(1, r)], writes=[("rc", r)])
                        S.add("dve", lambda e, r=r: e.tensor_tensor(out=rc[:, r, 2:3], in0=rc[:, r, 1:2], in1=lv[:, 5:6],
                                                                    op=ALU.mult),
                              reads=[("rc", r), ("lv", 5)], writes=[("rcn", r)])
                    for r in range(4):
                        S.add("act", lambda e, r=r: e.activation(out=t2[:, r, :], in_=po(1, r)[:, 0:128], func=AF.Copy,
                                                                 scale=rc[:, r, 2:3]),
                              reads=[pokey(1, r), ("rcn", r)], writes=[("t2", r)])
                    for r in range(4):
                        S.add("dve", lambda e, r=r: e.scalar_tensor_tensor(
                            out=ov[:, r, :], in0=po(0, r)[:, 0:128], scalar=rc[:, r, 0:1], in1=t2[:, r, :],
                            op0=ALU.mult, op1=ALU.add),
                            reads=[pokey(0, r), ("rc", r), ("t2", r)], writes=["ov"])

                    def step2(qc=qc, h=h):
                        S.add("dve", lambda e: e.tensor_tensor(out=osq, in0=ov, in1=ov, op=ALU.mult),
                              reads=["ov"], writes=["osq"])
                        S.add("dve", lambda e: e.reduce_sum(out=rs[:, 0:4], in_=osq, axis=AX.X),
                              reads=["osq"], writes=["rs"])
                        S.add("dve", lambda e: e.tensor_scalar(out=rs[:, 4:8], in0=rs[:, 0:4], scalar1=1.0 / 128,
                                                                scalar2=EPS, op0=ALU.mult, op1=ALU.add),
                              reads=["rs"], writes=["rs"])
                        S.add("act", lambda e: e.activation(out=rs[:, 8:12], in_=rs[:, 4:8], func=AF.Ln),
                              reads=["rs"], writes=["rs"])
                        S.add("act", lambda e: e.activation(out=rs[:, 12:16], in_=rs[:, 8:12], func=AF.Exp, scale=-0.5),
                              reads=["rs"], writes=["rs"])
                        S.add("dve", lambda e: e.tensor_tensor(
                            out=on, in0=ov, in1=rs[:, 12:16].unsqueeze(2).to_broadcast([128, 4, 128]), op=ALU.mult),
                            reads=["ov", "rs"], writes=["on"])
                        tpb = ps[7][:].bitcast(BF16)
                        S.add("pe", lambda e: [e.transpose(tpb[:, r * 128:(r + 1) * 128], on[:, r, :], ident[:])
                                               for r in range(4)][-1],
                              reads=["on", "ident"], writes=[("ps", 7)])
                        S.add("dve", lambda e: e.tensor_scalar(
                            out=yT2[:, h % 2, qc * 512:(qc + 1) * 512], in0=tpb[:, 0:512],
                            scalar1=cst[:, C_SUBLN:C_SUBLN + 1], scalar2=1.0 - LAMBDA_INIT,
                            op0=ALU.mult, op1=ALU.mult),
                            reads=[("ps", 7), "cst"], writes=[("yT", h % 2)])
                    pending2.append(step2)
                while pending2:
                    pending2.pop(0)()
                if h % 2 == 1 and getattr(cfg, "dbg", "") != "noattnwout":
                    wout_unit(yT2, wog2, 8 + h - 1)
            S.barrier()

        if getattr(cfg, "dbg", "") == "postattn":
            with nc.Block() as block:
                S.emit(nc, block, engsem)
            return nc
        if upto >= 3:
            al = Bump()
            wq = hTm[:, :, 0:1024]
            woc = hTm[:, :, 1024:2048]
            KmT = al([128, 8, MEM], BF16)
            Vm = al([128, 2, 1024], BF16)
            mark3 = al.off
            mT = al([128, KC, MEM], F32)
            mhT = al([128, KC, MEM], BF16)
            wkv = al([128, KC, 1024], BF16)
            s_cw = sem("s_cw")
            s_kv = sem("s_kv")
            s_m = sem("s_m")
            wcq_v = wcq_d.rearrange("(c p) n -> p c n", p=128)
            wck_v = wck_d.rearrange("(c p) n -> p c n", p=128)
            wcv_v = wcv_d.rearrange("(c p) n -> p c n", p=128)
            wco_v = wco_d.rearrange("(c p) n -> p c n", p=128)
            for hh in range(2):
                S.add("pool", lambda e, hh=hh: e.dma_start(out=wq[:, :, hh * 512:(hh + 1) * 512],
                                                            in_=wcq_v[:, :, hh * 512:(hh + 1) * 512]),
                      writes=["wq"], dma_sem=s_cw)
                S.add("pool", lambda e, hh=hh: e.dma_start(out=woc[:, :, hh * 512:(hh + 1) * 512],
                                                            in_=wco_v[:, :, hh * 512:(hh + 1) * 512]),
                      writes=["woc"], dma_sem=s_cw)
            S.add("sp", lambda e: e.dma_start(out=mT, in_=memT_d.rearrange("(c p) t -> p c t", p=128)),
                  writes=[("mT", c) for c in range(KC)], dma_sem=s_m)
            HC0 = 2048
            for t in range(NTO):
                o0 = t * 512
                rms_norm(lambda c, o0=o0: xT[:, c, o0:o0 + 512], [("x", t, c) for c in range(KC)], C_GC,
                         lambda c, o0=o0: hTm[:, c, HC0 + o0:HC0 + o0 + 512], [("hcT", t, c) for c in range(KC)])
            rms_norm(lambda c: mT[:, c, :], [("mT", c) for c in range(KC)], C_GMEM,
                     lambda c: mhT[:, c, :], [("mhT", c) for c in range(KC)], n=MEM)
            mk = [("mhT", c) for c in range(KC)]
            for hh in range(2):
                S.add("pool", lambda e, hh=hh: e.dma_start(out=wkv[:, :, hh * 512:(hh + 1) * 512],
                                                            in_=wck_v[:, :, hh * 512:(hh + 1) * 512]),
                      writes=["wkv"], dma_sem=s_kv)
            for kc in range(8):
                bk = psrot.next()
                S.add("pe", lambda e, bk=bk, kc=kc: [
                    e.matmul(ps[bk][:, 0:MEM], wkv[:, c, kc * 128:(kc + 1) * 128], mhT[:, c, :],
                             start=(c == 0), stop=(c == KC - 1)) for c in range(KC)][-1],
                    reads=["wkv"] + mk, writes=[("ps", bk)])
                S.add("act", lambda e, bk=bk, kc=kc: e.copy(out=KmT[:, kc, :], in_=ps[bk][:, 0:MEM]),
                      reads=[("ps", bk)], writes=["KmT"])
            for hh in range(2):
                S.add("pool", lambda e, hh=hh: e.dma_start(out=wkv[:, :, hh * 512:(hh + 1) * 512],
                                                            in_=wcv_v[:, :, hh * 512:(hh + 1) * 512]),
                      writes=["wkv"], dma_sem=s_kv)
            for mb in range(2):
                for hh in range(2):
                    bk = psrot.next()
                    S.add("pe", lambda e, bk=bk, mb=mb, hh=hh: [
                        e.matmul(ps[bk][:], mhT[:, c, mb * 128:(mb + 1) * 128], wkv[:, c, hh * 512:(hh + 1) * 512],
                                 start=(c == 0), stop=(c == KC - 1)) for c in range(KC)][-1],
                        reads=["wkv"] + mk, writes=[("ps", bk)])
                    S.add("act", lambda e, bk=bk, mb=mb, hh=hh: e.copy(out=Vm[:, mb, hh * 512:(hh + 1) * 512],
                                                                      in_=ps[bk][:]),
                          reads=[("ps", bk)], writes=["Vm"])
            S.barrier()
            al.off = mark3
            ocT = al([128, 8, 512], BF16)
            XB = [dict(qT=al([128, 2, 512], BF16), PTc=al([128, 2, 512], BF16), rec=al([128, 512], F32))
                  for _ in range(2)]

            class Seg3:
                def __init__(self):
                    self.ops = []

                def add(self, eng, fn, reads=(), writes=()):
                    self.ops.append((eng, fn, list(reads), list(writes)))

            def flush3(segs):
                n = max([len(sg.ops) for sg in segs] + [0])
                for i in range(n):
                    for sg in segs:
                        if i < len(sg.ops):
                            eng, fn, r, w = sg.ops[i]
                            S.add(eng, fn, reads=r, writes=w)

            def head_seg(ch, x, hk, hcT):
                X = XB[x]
                pb = [4 * x + i for i in range(4)]
                sg = Seg3()
                for ee in range(2):
                    col = (ch * 2 + ee) * 128
                    sg.add("pe", lambda e, ee=ee, col=col: [
                        e.matmul(ps[pb[ee]][:], wq[:, c, col:col + 128], hcT[:, c, :],
                                 start=(c == 0), stop=(c == KC - 1)) for c in range(KC)][-1],
                        reads=["wq"] + hk, writes=[("ps", pb[ee])])
                    sg.add("act", lambda e, ee=ee: e.copy(out=X["qT"][:, ee, :], in_=ps[pb[ee]][:]),
                           reads=[("ps", pb[ee])], writes=[("qT", x, ee)])
                for mb in range(2):
                    sg.add("pe", lambda e, mb=mb: [
                        e.matmul(ps[pb[2 + mb]][:], KmT[:, ch * 2 + ee, mb * 128:(mb + 1) * 128], X["qT"][:, ee, :],
                                 start=(ee == 0), stop=(ee == 1)) for ee in range(2)][-1],
                        reads=["KmT", ("qT", x, 0), ("qT", x, 1)], writes=[("ps", pb[2 + mb])])
                    sg.add("act", lambda e, mb=mb: e.activation(out=X["PTc"][:, mb, :], in_=ps[pb[2 + mb]][:],
                                                                func=AF.Exp, scale=1.0 / 16),
                           reads=[("ps", pb[2 + mb])], writes=[("PTc", x, mb)])
                sg.add("pe", lambda e: [
                    e.matmul(ps[pb[0]][:], onesb[:], X["PTc"][:, mb, :], start=(mb == 0), stop=(mb == 1))
                    for mb in range(2)][-1],
                    reads=["onesb", ("PTc", x, 0), ("PTc", x, 1)], writes=[("ps", pb[0])])
                sg.add("dve", lambda e: e.reciprocal(out=X["rec"], in_=ps[pb[0]][:]),
                       reads=[("ps", pb[0])], writes=[("rec", x)])
                for ee in range(2):
                    col = ch * 256 + ee * 128
                    sg.add("pe", lambda e, ee=ee, col=col: [
                        e.matmul(ps[pb[1 + ee]][:], Vm[:, mb, col:col + 128], X["PTc"][:, mb, :],
                                 start=(mb == 0), stop=(mb == 1)) for mb in range(2)][-1],
                        reads=["Vm", ("PTc", x, 0), ("PTc", x, 1)], writes=[("ps", pb[1 + ee])])
                    sg.add("dve", lambda e, ee=ee: e.tensor_tensor(
                        out=ocT[:, ch * 2 + ee, :], in0=ps[pb[1 + ee]][:], in1=X["rec"], op=ALU.mult),
                        reads=[("ps", pb[1 + ee]), ("rec", x)], writes=[("ocT", ch * 2 + ee)])
                return sg

            for t in range(NTO):
                o0 = t * 512
                hcT = hTm[:, :, HC0 + o0:HC0 + o0 + 512]
                hk = [("hcT", t, c) for c in range(KC)]
                for chp in range(2):
                    flush3([head_seg(2 * chp, 0, hk, hcT), head_seg(2 * chp + 1, 1, hk, hcT)])
                for o in range(KC):
                    bk = psrot.next()
                    S.add("pe", lambda e, bk=bk, o=o: [
                        e.matmul(ps[bk][:], woc[:, j, o * 128:(o + 1) * 128], ocT[:, j, :],
                                 start=(j == 0), stop=(j == 7)) for j in range(8)][-1],
                        reads=["woc"] + [("ocT", j) for j in range(8)], writes=[("ps", bk)])
                    resid_add(bk, o, o0, 1.0)
            S.barrier()

        if upto >= 9:
            ffn_phase([(None, t * T) for t in range(NOWN // T)], w2i_d, w2o_d, C_G2)
            S.barrier()

        al = Bump()
        ot = [al([128, 512], F32) for _ in range(2)]
        otrot = Rot(range(2))
        for s in range(NTO):
            o0 = s * 512
            for c in range(KC):
                S.add("act", lambda e, c=c, o0=o0: e.activation(out=sq[:, c, :], in_=xT[:, c, o0:o0 + 512],
                                                                  func=AF.Square),
                      reads=[("x", s, c)], writes=[("sq", c)])
            bk = psrot.next()
            S.add("pe", lambda e, bk=bk: [e.matmul(ps[bk][:], onesm[:], sq[:, c, :], start=(c == 0),
                                                   stop=(c == KC - 1)) for c in range(KC)][-1],
                  reads=["onesm"] + [("sq", c) for c in range(KC)], writes=[("ps", bk)])
            rstd_from_ps(bk, 512, rstd[:])
            for c in range(KC):
                k = otrot.next()
                S.add("dve", lambda e, c=c, k=k, o0=o0: e.scalar_tensor_tensor(
                    out=ot[k], in0=xT[:, c, o0:o0 + 512], scalar=cst[:, C_GF + c:C_GF + c + 1], in1=rstd[:],
                    op0=ALU.mult, op1=ALU.mult),
                    reads=[("x", s, c), "rstd", "cst"], writes=[("ot", k)])
                S.add("sp", lambda e, c=c, k=k, o0=o0: e.dma_start(out=out_v[:, c, o0:o0 + 512], in_=ot[k]),
                      reads=[("ot", k)], writes=[("out", s, c)], dma_sem=s_out[k])
        S.add("sp", lambda e: e.nop(), reads=[("out", s, c) for s in range(NTO) for c in range(KC)])

        with nc.Block() as block:
            S.emit(nc, block, engsem)
    return nc


def _pcols(v):
    return np.ascontiguousarray(np.asarray(v, np.float32).reshape(-1, 128).T)


def _rep(v):
    v = np.asarray(v, np.float32)
    return np.broadcast_to(v[None, :], (128, len(v)))


def _rel_bucket(rel):
    n = np.maximum(-rel, 0)
    nf = np.maximum(n, 1).astype(np.float32)
    large = 16 + (np.log(nf / np.float32(16)) / np.float32(np.log(128 / 16)) * np.float32(16)).astype(np.int32)
    large = np.minimum(large, 31)
    return np.where(n < 16, n, large)


def make_inputs(cfg, inp, n_batch):
    Sq = inp["x"].shape[1]
    assert cfg.NPRE == cfg.NOWN == Sq // 2
    H = Sq // 2
    f32 = np.float32
    cst = np.zeros((128, NCST), f32)
    cst[:, C_G1:C_G1 + 8] = _pcols(inp["norm_ffn1_w"][0])
    cst[:, C_GF:C_GF + 8] = _pcols(inp["norm_final_w"])
    cst[:, C_G2:C_G2 + 8] = _pcols(inp["norm_ffn2_w"][0])
    cst[:, C_GM:C_GM + 8] = _pcols(inp["norm_mix_w"][0])
    cst[:, C_GC:C_GC + 8] = _pcols(inp["norm_cross_w"][0])
    cst[:, C_GMEM:C_GMEM + 8] = _pcols(inp["norm_mem_w"][0])
    cst[:, C_SSDW:C_SSDW + 8] = _pcols(inp["ssd_norm_w"][0])
    cst[:, C_SUBLN] = np.asarray(inp["subln_w"][0], f32)
    cst[:, C_CONVB:C_CONVB + 16] = _pcols(inp["conv_b"][0])
    cw = np.asarray(inp["conv_w"][0], f32)
    cst[:, C_CONVW:C_CONVW + 64] = cw.reshape(4, 16, 128).transpose(2, 1, 0).reshape(128, 64)
    cst[:, C_DTB:C_DTB + 16] = _rep(inp["dt_bias"][0])
    cst[:, C_ALOG:C_ALOG + 16] = _rep(inp["a_log"][0])
    cst[:, C_DSKIP:C_DSKIP + 16] = _rep(inp["d_skip"][0])
    cst[:, C_LQ1:C_LQ1 + 64] = _rep(inp["lambda_q1"][0])
    cst[:, C_LK1:C_LK1 + 64] = _rep(inp["lambda_k1"][0])
    cst[:, C_LQ2:C_LQ2 + 64] = _rep(inp["lambda_q2"][0])
    cst[:, C_LK2:C_LK2 + 64] = _rep(inp["lambda_k2"][0])
    rb = np.asarray(inp["rel_bias"], f32)
    cst[:, C_FAROWN:C_FAROWN + 8] = _rep(rb[31])
    rbx = np.concatenate([rb, np.full((1, 8), NEG, f32)], axis=0)
    k = np.arange(128)[:, None]
    q = np.arange(128)[None, :]
    idxD = np.where(k <= q, _rel_bucket(k - q), 32)
    idxP = _rel_bucket(k - q - 128)
    Dt = rbx[idxD]
    Pt = rbx[idxP]
    negt = np.full_like(Pt, NEG)
    shared = {}
    for nm in ("ffn1_w_in", "ffn1_w_out", "ffn2_w_in", "ffn2_w_out", "w_in_mix", "w_out_mix",
               "w_cq", "w_ck", "w_cv", "w_co"):
        shared[nm] = np.ascontiguousarray(inp[nm][0], dtype=f32)
    maps = []
    for b in range(n_batch):
        xb = np.asarray(inp["x"][b], f32)
        memT = np.ascontiguousarray(np.asarray(inp["mem"][b], f32).T)
        for half in range(2):
            own = xb[half * H:(half + 1) * H]
            pre = xb[0:H]
            m = dict(shared)
            m["xT"] = np.ascontiguousarray(np.concatenate([pre, own], axis=0).T)
            m["memT"] = memT
            c2 = cst.copy()
            c2[:, C_FLAG] = float(half)
            c2[:, C_FARPRE:C_FARPRE + 8] = _rep(rb[31]) if half == 1 else NEG
            m["cst"] = c2
            Pb = Pt if half == 1 else negt
            bt = np.stack([Dt, Pt, Pb], axis=0)
            m["biasT"] = np.ascontiguousarray(bt.transpose(1, 3, 0, 2).reshape(128, 8 * 3 * 128), dtype=f32)
            maps.append(m)
    return maps


def build_dbg(cfg):
    try:
        return build(cfg)
    except _Stop:
        return cfg._nc


def run(cfg, inp, n_batch, trace=False):
    nc = build_dbg(cfg)
    maps = make_inputs(cfg, inp, n_batch)
    n = len(maps)
    res = run_bass_kernel_spmd(nc, maps, core_ids=list(range(n)), trace=trace)
    H = cfg.NOWN
    out = np.zeros((n_batch, 2 * H, cfg.D), np.float32)
    for i, r in enumerate(res.results):
        b, half = i // 2, i % 2
        out[b, half * H:(half + 1) * H] = np.asarray(r["outT"]).T
    return out, res


def kernel(**inputs):
    cfg = Cfg()
    out, _ = run(cfg, inputs, 4)
    return out
```
